# Optimizing a Trainium2 kernel written in Bass

```python
import math
import jax, jax.numpy as jnp
from jax import lax
import numpy as np

D_MODEL = 1024
BATCH = 32
SEQ = 2048
DEPTH = 1
DEC_BATCH = 8
DEC_SEQ = 2048
PAST_LEN = 128

RET_HEADS = 4
RET_V_DIM = D_MODEL // RET_HEADS
RET_QK_DIM = RET_V_DIM // 2
RET_CHUNK = 128
ROPE_BASE = 10000.0
DIFF_HEADS = 8
DIFF_V_DIM = D_MODEL // DIFF_HEADS
DIFF_SUB_DIM = DIFF_V_DIM // 2
Q_BLOCK = 128
N_BUCKETS = 32
MAX_DISTANCE = 128
D_FF = ((8 * D_MODEL // 3 + 255) // 256) * 256
EPS = 1e-6

COLS = (RET_HEADS * RET_QK_DIM, RET_HEADS * RET_QK_DIM, RET_HEADS * RET_V_DIM, RET_HEADS * RET_V_DIM,
        DIFF_HEADS * 2 * DIFF_SUB_DIM, DIFF_HEADS * 2 * DIFF_SUB_DIM, DIFF_HEADS * DIFF_V_DIM, 2 * D_MODEL)
IN_COLS = sum(COLS)
SPLITS = tuple(int(s) for s in np.cumsum(COLS)[:-1])

kernel_name = "hybrid_retention_diffattn_encoder"


def rms_norm(x, g):
    xf = x.astype(jnp.float32)
    y = xf * lax.rsqrt(jnp.mean(xf * xf, axis=-1, keepdims=True) + EPS)
    return (y * g.astype(jnp.float32)).astype(x.dtype)


def rms_unit(xf):
    return xf * lax.rsqrt(jnp.mean(xf * xf, axis=-1, keepdims=True) + EPS)


def rotary(x):
    S, d = x.shape[1], x.shape[-1]
    half = d // 2
    inv = ROPE_BASE ** (-jnp.arange(half, dtype=jnp.float32) / half)
    ang = jnp.arange(S, dtype=jnp.float32)[:, None] * inv[None, :]
    cos = jnp.cos(ang)[None, :, None, :]
    sin = jnp.sin(ang)[None, :, None, :]
    x1, x2 = x[..., :half], x[..., half:]
    return jnp.concatenate([x1 * cos - x2 * sin, x1 * sin + x2 * cos], axis=-1)


def retention_dir(q, k, v, log_gamma, inclusive):
    B, S, H, dk = q.shape
    dv = v.shape[-1]
    C = RET_CHUNK
    N = S // C
    to_chunks = lambda t: t.reshape(B, N, C, H, t.shape[-1]).transpose(1, 0, 3, 2, 4)
    qc, kc, vc = to_chunks(q), to_chunks(k), to_chunks(v)
    pos = jnp.arange(C, dtype=jnp.float32)
    diff = pos[:, None] - pos[None, :]
    mask = (diff >= 0) if inclusive else (diff > 0)
    lg = log_gamma[:, None, None]
    dmat = jnp.where(mask[None], jnp.exp(lg * jnp.maximum(diff, 0.0)[None]), 0.0)
    xi = jnp.exp(log_gamma[:, None] * (pos + 1.0)[None, :])[None, :, :, None]
    zeta = jnp.exp(log_gamma[:, None] * (C - 1.0 - pos)[None, :])[None, :, :, None]
    chunk_decay = jnp.exp(log_gamma * C)[None, :, None, None]

    def step(state, inp):
        qi, ki, vi = inp
        scores = jnp.einsum('bhik,bhjk->bhij', qi, ki) * dmat[None]
        intra = jnp.einsum('bhij,bhjv->bhiv', scores, vi)
        inter = jnp.einsum('bhik,bhkv->bhiv', qi, state) * xi
        new_state = state * chunk_decay + jnp.einsum('bhjk,bhjv->bhkv', ki * zeta, vi)
        return new_state, intra + inter

    state0 = jnp.zeros((B, H, dk, dv), jnp.float32)
    _, out = lax.scan(step, state0, (qc, kc, vc))
    return out.transpose(1, 0, 3, 2, 4).reshape(B, S, H, dv)


def t5_bucket(rel):
    nb = N_BUCKETS // 2
    max_exact = nb // 2
    ret = jnp.where(rel > 0, nb, 0)
    n = jnp.abs(rel)
    nf = jnp.maximum(n, 1).astype(jnp.float32)
    large = max_exact + (jnp.log(nf / max_exact) / math.log(MAX_DISTANCE / max_exact)
                         * (nb - max_exact)).astype(jnp.int32)
    large = jnp.minimum(large, nb - 1)
    return ret + jnp.where(n < max_exact, n, large)


def diff_attention(q, k, v, lam, table):
    B, S, H, _, dh = q.shape
    NB = S // Q_BLOCK
    qb = q.reshape(B, NB, Q_BLOCK, H, 2, dh).transpose(1, 0, 2, 3, 4, 5)
    starts = jnp.arange(NB, dtype=jnp.int32) * Q_BLOCK
    kpos = jnp.arange(S, dtype=jnp.int32)

    def block(args):
        qblk, start = args
        qpos = start + jnp.arange(Q_BLOCK, dtype=jnp.int32)
        bias = table[t5_bucket(kpos[None, :] - qpos[:, None])].transpose(2, 0, 1)
        logits = jnp.einsum('bqhcd,bkhcd->bhcqk', qblk, k).astype(jnp.float32) + bias[None, :, None].astype(jnp.float32)
        p = jax.nn.softmax(logits, axis=-1)
        attn = p[:, :, 0] - lam * p[:, :, 1]
        return jnp.einsum('bhqk,bkhv->bqhv', attn, v)

    out = lax.map(block, (qb, starts))
    return out.transpose(1, 0, 2, 3, 4).reshape(B, S, H, v.shape[-1])


def encoder_layer(x, layer, rel_bias_table, norm_mix_g, w_in, ret_decay_fwd, ret_decay_bwd,
                  q_norm_g, k_norm_g, lam_q1, lam_k1, lam_q2, lam_k2, subln_g, w_out,
                  norm_ffn_g, w_gate, w_up, w_down):
    B, S, _ = x.shape
    f32 = jnp.float32
    h = rms_norm(x, norm_mix_g)
    proj = h @ w_in
    rq, rk, rv, rg, dq, dk, dv, mg = jnp.split(proj, SPLITS, axis=-1)

    rq = rotary(rq.reshape(B, S, RET_HEADS, RET_QK_DIM).astype(f32))
    rk = rotary(rk.reshape(B, S, RET_HEADS, RET_QK_DIM).astype(f32)) * (RET_QK_DIM ** -0.5)
    rv = rv.reshape(B, S, RET_HEADS, RET_V_DIM).astype(f32)
    log_gf = jnp.log1p(-jnp.exp(ret_decay_fwd.astype(f32)))
    log_gb = jnp.log1p(-jnp.exp(ret_decay_bwd.astype(f32)))
    ret_f = retention_dir(rq, rk, rv, log_gf, True)
    ret_b = retention_dir(rq[:, ::-1], rk[:, ::-1], rv[:, ::-1], log_gb, False)[:, ::-1]
    ret = rms_unit(ret_f + ret_b).reshape(B, S, D_MODEL)
    ret_out = jax.nn.silu(rg.astype(f32)) * ret

    lambda_init = 0.8 - 0.6 * math.exp(-0.3 * layer)
    dq = rms_norm(dq.reshape(B, S, DIFF_HEADS, 2, DIFF_SUB_DIM), q_norm_g) * (DIFF_SUB_DIM ** -0.5)
    dk = rms_norm(dk.reshape(B, S, DIFF_HEADS, 2, DIFF_SUB_DIM), k_norm_g)
    dv = dv.reshape(B, S, DIFF_HEADS, DIFF_V_DIM).astype(f32)
    lam = (jnp.exp(jnp.sum(lam_q1.astype(f32) * lam_k1.astype(f32)))
           - jnp.exp(jnp.sum(lam_q2.astype(f32) * lam_k2.astype(f32))) + lambda_init)
    o = diff_attention(dq, dk, dv, lam, rel_bias_table)
    diff_out = (rms_norm(o, subln_g) * (1.0 - lambda_init)).reshape(B, S, D_MODEL)

    gates = jax.nn.sigmoid(mg.astype(f32))
    merged = gates[..., :D_MODEL] * ret_out + gates[..., D_MODEL:] * diff_out
    x = x + merged.astype(x.dtype) @ w_out

    h = rms_norm(x, norm_ffn_g)
    return x + (jax.nn.silu(h @ w_gate) * (h @ w_up)) @ w_down


def trunk(x, rel_bias_table, norm_mix_g, w_in, ret_decay_fwd, ret_decay_bwd, q_norm_g, k_norm_g,
          lam_q1, lam_k1, lam_q2, lam_k2, subln_g, w_out, norm_ffn_g, w_gate, w_up, w_down):
    for l in range(DEPTH):
        x = encoder_layer(x, l, rel_bias_table, norm_mix_g[l], w_in[l], ret_decay_fwd[l], ret_decay_bwd[l],
                          q_norm_g[l], k_norm_g[l], lam_q1[l], lam_k1[l], lam_q2[l], lam_k2[l], subln_g[l],
                          w_out[l], norm_ffn_g[l], w_gate[l], w_up[l], w_down[l])
    return x


def setup_inputs(seed: int = 0) -> dict:
    key = jax.random.key(seed)
    ks = jax.random.split(key, 20)
    nrm = lambda k, shape, s: jax.random.normal(k, shape, jnp.float32) * s
    base_decay = jnp.log(jnp.exp(jnp.linspace(math.log(1.0 / 32), math.log(1.0 / 512), RET_HEADS))).astype(jnp.float32)
    return {
        "x_prompt": nrm(ks[0], (BATCH, SEQ, D_MODEL), 1.0),
        "x_sample": nrm(ks[1], (DEC_BATCH, DEC_SEQ, D_MODEL), 1.0),
        "rel_bias_table": nrm(ks[2], (N_BUCKETS, DIFF_HEADS), 0.5),
        "norm_mix_g": 1.0 + nrm(ks[3], (DEPTH, D_MODEL), 0.02),
        "w_in": nrm(ks[4], (DEPTH, D_MODEL, IN_COLS), D_MODEL ** -0.5),
        "ret_decay_fwd": base_decay[None] + nrm(ks[5], (DEPTH, RET_HEADS), 0.05),
        "ret_decay_bwd": base_decay[None] + nrm(ks[6], (DEPTH, RET_HEADS), 0.05),
        "q_norm_g": 1.0 + nrm(ks[7], (DEPTH, DIFF_SUB_DIM), 0.02),
        "k_norm_g": 1.0 + nrm(ks[8], (DEPTH, DIFF_SUB_DIM), 0.02),
        "lam_q1": nrm(ks[9], (DEPTH, DIFF_SUB_DIM), 0.1),
        "lam_k1": nrm(ks[10], (DEPTH, DIFF_SUB_DIM), 0.1),
        "lam_q2": nrm(ks[11], (DEPTH, DIFF_SUB_DIM), 0.1),
        "lam_k2": nrm(ks[12], (DEPTH, DIFF_SUB_DIM), 0.1),
        "subln_g": 1.0 + nrm(ks[13], (DEPTH, DIFF_V_DIM), 0.02),
        "w_out": nrm(ks[14], (DEPTH, D_MODEL, D_MODEL), D_MODEL ** -0.5),
        "norm_ffn_g": 1.0 + nrm(ks[15], (DEPTH, D_MODEL), 0.02),
        "w_gate": nrm(ks[16], (DEPTH, D_MODEL, D_FF), D_MODEL ** -0.5),
        "w_up": nrm(ks[17], (DEPTH, D_MODEL, D_FF), D_MODEL ** -0.5),
        "w_down": nrm(ks[18], (DEPTH, D_FF, D_MODEL), D_FF ** -0.5),
    }


def reference(x_prompt, x_sample, rel_bias_table, norm_mix_g, w_in, ret_decay_fwd, ret_decay_bwd,
              q_norm_g, k_norm_g, lam_q1, lam_k1, lam_q2, lam_k2, subln_g, w_out, norm_ffn_g,
              w_gate, w_up, w_down):
    y_prompt = trunk(x_prompt, rel_bias_table, norm_mix_g, w_in, ret_decay_fwd, ret_decay_bwd, q_norm_g,
                     k_norm_g, lam_q1, lam_k1, lam_q2, lam_k2, subln_g, w_out, norm_ffn_g, w_gate, w_up, w_down)
    y_sample = trunk(x_sample, rel_bias_table, norm_mix_g, w_in, ret_decay_fwd, ret_decay_bwd, q_norm_g,
                     k_norm_g, lam_q1, lam_k1, lam_q2, lam_k2, subln_g, w_out, norm_ffn_g, w_gate, w_up, w_down)
    return (y_prompt, y_sample)
```

```python
import math
import os
import numpy as np
import ml_dtypes
import concourse.bass as bass
import concourse.mybir as mybir
from concourse.bass_utils import run_bass_kernel_spmd

F32 = mybir.dt.float32
BF16 = mybir.dt.bfloat16
AF = mybir.ActivationFunctionType
ALU = mybir.AluOpType
AX = mybir.AxisListType

NCORES = 8
S = 2048
D = 1024
NT = 16
DFF = 2816
NFC = 22
EPS = 1e-6
LAMBDA_INIT = 0.8 - 0.6 * math.exp(-0.3 * 0)

C_RQ, C_RK, C_RV, C_RG, C_DQ, C_DK, C_DV, C_MGR, C_MGD = 0, 512, 1024, 2048, 3072, 4096, 5120, 6144, 7168


class _Op:
    __slots__ = ("eng", "fn", "deps", "dma_key", "dma_cnt", "ms", "has_dep")

    def __init__(self, eng, fn, deps, dma_key):
        self.eng = eng
        self.fn = fn
        self.deps = deps
        self.dma_key = dma_key
        self.dma_cnt = 0
        self.ms = 0
        self.has_dep = False


class Sched:
    ENGS = ("pe", "act", "dve", "pool", "sp")

    def __init__(self):
        self.ops = []
        self.lw = {}
        self.rd = {}
        self.roots = {}
        self.dma_total = {}
        self.last_on_eng = {e: None for e in self.ENGS}
        self.barrier_deps = {e: set() for e in self.ENGS}
        self.dma_last = {}
        self.dma_hist = {}

    def _related(self, key):
        ks = self.roots.get(key[0])
        if not ks:
            return ()
        n = len(key)
        out = []
        for k in ks:
            m = min(n, len(k))
            if k[:m] == key[:m]:
                out.append(k)
        return out

    def add(self, eng, fn, r=(), w=(), dma_key=None):
        idx = len(self.ops)
        deps = set(self.barrier_deps[eng])
        self.barrier_deps[eng] = set()
        for key in r:
            for k in self._related(key):
                lw = self.lw.get(k)
                if lw is not None:
                    deps.add(lw)
        for key in w:
            for k in self._related(key):
                lw = self.lw.get(k)
                if lw is not None:
                    deps.add(lw)
                rr = self.rd.get(k)
                if rr:
                    deps.update(rr.values())
        is_dma = dma_key is not None
        for key in r:
            self.roots.setdefault(key[0], set()).add(key)
            d = self.rd.setdefault(key, {})
            if is_dma:
                d[("dma", idx)] = idx
            else:
                d[eng] = idx
        for key in w:
            self.roots.setdefault(key[0], set()).add(key)
            for k in self._related(key):
                if len(k) > len(key):
                    self.lw[k] = idx
                    self.rd[k] = {}
            self.lw[key] = idx
            self.rd[key] = {}
        deps.discard(idx)
        if eng == "pe" and os.environ.get("KPE", "0") == "0":
            deps = set(j for j in deps if not (self.ops[j].eng == "pe" and self.ops[j].dma_key is None))
        op = _Op(eng, fn, deps, dma_key)
        if is_dma:
            self.dma_total[dma_key] = self.dma_total.get(dma_key, 0) + 16
            op.dma_cnt = self.dma_total[dma_key]
            self.dma_last[dma_key] = idx
            self.dma_hist.setdefault(dma_key, []).append((idx, op.dma_cnt))
        self.ops.append(op)
        self.last_on_eng[eng] = idx
        return idx

    def barrier(self):
        lasts = set(i for i in self.last_on_eng.values() if i is not None)
        lasts.update(self.dma_last.values())
        for e in self.ENGS:
            self.barrier_deps[e].update(lasts)

    def emit(self, nc, stack):
        ops = self.ops
        for op in ops:
            for j in op.deps:
                ops[j].has_dep = True
        cnt = {e: 0 for e in self.ENGS}
        for op in ops:
            if op.dma_key is None and op.has_dep:
                cnt[op.eng] += 1
                op.ms = cnt[op.eng]
        esem = {e: stack.enter_context(nc.semaphore("sem_" + e)) for e in self.ENGS}
        dsem = {k: stack.enter_context(nc.semaphore("dsem_%d" % i)) for i, k in enumerate(self.dma_total)}
        per_eng = {e: [] for e in self.ENGS}
        for op in ops:
            per_eng[op.eng].append(op)

        import bisect
        opidx = {id(op): i for i, op in enumerate(ops)}

        if os.environ.get("KCHECK", "0") == "1":
            done = [False] * len(ops)
            ptr = {e: 0 for e in self.ENGS}
            progress = True
            while progress:
                progress = False
                for e in self.ENGS:
                    while ptr[e] < len(per_eng[e]):
                        op = per_eng[e][ptr[e]]
                        if all(done[j] for j in op.deps):
                            done[opidx[id(op)]] = True
                            ptr[e] += 1
                            progress = True
                        else:
                            break
            for e in self.ENGS:
                if ptr[e] < len(per_eng[e]):
                    op = per_eng[e][ptr[e]]
                    print("DEADLOCK", e, ptr[e], len(per_eng[e]), "op", opidx[id(op)], "waits", [(j, ops[j].eng) for j in op.deps if not done[j]])
            print("KCHECK ok", {e: len(per_eng[e]) for e in self.ENGS})

        def run(eng_name, e):
            waited = {}
            for op in per_eng[eng_name]:
                need = {}
                me = opidx[id(op)]
                for j in op.deps:
                    d = ops[j]
                    if d.dma_key is not None:
                        hist = self.dma_hist[d.dma_key]
                        pos = bisect.bisect_left(hist, (me, 0)) - 1
                        sem, val = dsem[d.dma_key], max(d.dma_cnt, hist[pos][1] if pos >= 0 else 0)
                    else:
                        sem, val = esem[d.eng], d.ms
                    kk = id(sem)
                    if val > need.get(kk, (None, 0))[1]:
                        need[kk] = (sem, val)
                for kk, (sem, val) in need.items():
                    if waited.get(kk, 0) < val:
                        e.wait_ge(sem, val)
                        waited[kk] = val
                if op.fn is None:
                    continue
                ins = op.fn(e)
                if op.dma_key is not None:
                    ins.then_inc(dsem[op.dma_key], 16)
                elif op.has_dep:
                    ins.then_inc(esem[eng_name], 1)

        block = stack.enter_context(nc.Block())

        @block.tensor
        def _(e):
            run("pe", e)

        @block.scalar
        def _(e):
            run("act", e)

        @block.vector
        def _(e):
            run("dve", e)

        @block.gpsimd
        def _(e):
            run("pool", e)

        @block.sync
        def _(e):
            run("sp", e)


class Rot:
    def __init__(self, items):
        self.items = list(items)
        self.i = 0

    def next(self):
        v = self.items[self.i % len(self.items)]
        self.i += 1
        return v


def _t5_bucket_np(rel):
    rel = np.asarray(rel, dtype=np.int64)
    nb = 16
    max_exact = 8
    ret = np.where(rel > 0, nb, 0)
    n = np.abs(rel)
    nf = np.maximum(n, 1).astype(np.float32)
    large = max_exact + (np.log(nf / np.float32(max_exact)) / np.float32(math.log(128 / max_exact))
                         * np.float32(nb - max_exact)).astype(np.int32)
    for nn, b in ((16, 10), (32, 12), (64, 14)):
        large = np.where(n == nn, b, large)
    large = np.minimum(large, nb - 1)
    return ret + np.where(n < max_exact, n, large)


def _host_consts():
    bf = ml_dtypes.bfloat16
    c = {}
    eye = np.eye(128, dtype=np.float32)
    c["ident"] = eye.astype(bf)
    c["antii"] = eye[::-1].copy().astype(bf)
    p = np.arange(128)
    blk = ((p[:, None] // 64) == (p[None, :] // 64)).astype(np.float32) / 64.0
    c["blk"] = blk.astype(bf)
    half = 64
    inv = (10000.0 ** (-np.arange(half, dtype=np.float32) / half)).astype(np.float32)
    ang = np.arange(S, dtype=np.float32)[None, :] * inv[:, None]
    cos = np.cos(ang).astype(np.float32)
    sin = np.sin(ang).astype(np.float32)
    cs = np.zeros((128, 2, S), np.float32)
    cs[:64, 0] = cos
    cs[64:, 0] = cos
    cs[:64, 1] = -sin
    cs[64:, 1] = sin
    c["cs"] = cs.astype(bf)
    t = np.arange(767)
    b = _t5_bucket_np(383 - t)
    oh = np.zeros((32, 767), np.float32)
    oh[b, t] = 1.0
    c["oh1"] = oh
    cf = np.zeros((128, 16 + 256 + 256 + 640 + 2 + 256), np.float32)
    cf[:, 0:16] = (128.0 * np.arange(16))[None, :]
    jl = np.arange(128)[:, None].astype(np.float32)
    il = np.arange(256)[None, :].astype(np.float32)
    cf[:, 16:272] = il - jl + 128.0
    cf[:, 272:528] = jl - il + 256.0
    cc = np.arange(640)[None, :].astype(np.float32)
    cf[:, 528:1168] = jl - cc + 256.0
    cf[:, 1168] = 127.0 - np.arange(128)
    cf[:, 1169] = np.arange(128)
    cf[:, 1170:1298] = (np.arange(128) + 1.0)[None, :]
    cf[:, 1298:1426] = (128.0 - np.arange(128))[None, :]
    c["cf32"] = cf
    m01 = np.zeros((128, 2), np.float32)
    m01[:64, 0] = 1.0
    m01[64:, 1] = 1.0
    c["m01"] = m01
    return c


def build_program(nseq):
    from contextlib import ExitStack
    nc = bass.Bass("TRN2", target_bir_lowering=False)
    stack = ExitStack()
    sch = Sched()

    def din(name, shape, dt=F32):
        return nc.dram_tensor(name, list(shape), dt, kind="ExternalInput").ap()

    x_d = din("x", [nseq, S, D])
    w_in_d = din("w_in", [D, 8192])
    w_out_d = din("w_out", [D, D])
    w_gate_d = din("w_gate", [D, DFF])
    w_up_d = din("w_up", [D, DFF])
    w_down_d = din("w_down", [DFF, D])
    ident_d = din("ident", [128, 128], BF16)
    antii_d = din("antii", [128, 128], BF16)
    blk_d = din("blk", [128, 128], BF16)
    cs_d = din("cs", [128, 2, S], BF16)
    oh1_d = din("oh1", [32, 767])
    cf32_d = din("cf32", [128, 1426])
    m01_d = din("m01", [128, 2])
    tab_d = din("tab", [32, 8])
    tabb_d = din("tabb", [128, 256])
    dec_d = din("dec", [128, 8])
    lamv_d = din("lamv", [128, 4, 64])
    gqk_d = din("gqk", [128, 2])
    subln_d = din("subln", [128, 256])
    g1t_d = din("g1t", [128, 8])
    g2t_d = din("g2t", [128, 8])
    y_d = nc.dram_tensor("y", [nseq, S, D], F32, kind="ExternalOutput").ap()

    win_s = nc.dram_tensor("win_s", [20, D, 512], BF16, kind="Internal").ap()
    wo_s = nc.dram_tensor("wo_s", [2, D, 512], BF16, kind="Internal").ap()
    wgu_s = nc.dram_tensor("wgu_s", [11, D, 512], BF16, kind="Internal").ap()
    wd_s = nc.dram_tensor("wd_s", [2, DFF, 512], BF16, kind="Internal").ap()
    gvb_t = nc.dram_tensor("gv_s", [8, 767], F32, kind="Internal")
    gv_s = gvb_t.ap()

    TOT = 204 * 1024
    raw = nc.alloc_sbuf_tensor("raw", [128, TOT // 2], BF16)
    cur = [0]

    def alloc(nbytes, dt, pattern=None, base=None, **kw):
        if base is None:
            off = cur[0]
            cur[0] += (nbytes + 63) // 64 * 64
            assert cur[0] <= TOT, ("sbuf overflow", cur[0])
        else:
            off = base
        v = raw[:, off // 2:(off + nbytes) // 2]
        if dt is F32:
            v = v.bitcast(F32)
        if pattern:
            v = v.rearrange(pattern, **kw)
        return v

    ident = alloc(256, BF16)
    antii = alloc(256, BF16)
    blk = alloc(256, BF16)
    cs = alloc(2 * S * 2, BF16, "p (a n) -> p a n", a=2)
    dn = alloc(4 * 640 * 2, BF16, "p (h c) -> p h c", h=4)
    Xi = alloc(8 * 128 * 2, BF16, "p (h c) -> p h c", h=8)
    zt = alloc(8 * 4, F32)
    cdc = alloc(8 * 4, F32)
    pws = alloc(8 * 16 * 4, F32, "p (h m) -> p h m", h=8)
    tabb = alloc(256 * 4, F32)
    lg = alloc(16 * 4, F32)
    gcols = alloc(8 * 4, F32)
    lamc = alloc(8 * 4, F32)
    sg8 = alloc(256 * 4, F32)
    g1t = alloc(8 * 4, F32)
    g2t = alloc(8 * 4, F32)
    stat = alloc(64 * 4, F32)
    hT = alloc(8 * S * 2, BF16, "p (k n) -> p k n", k=8)
    merged_base = cur[0]
    merged = alloc(NT * D * 2, BF16, "p (t c) -> p t c", t=NT)
    wring = [alloc(8 * 512 * 2, BF16, "p (k c) -> p k c", k=8) for _ in range(2)]
    U0 = cur[0]
    print("fixed sbuf bytes", U0, "U bytes", TOT - U0)

    cur[0] = U0
    E = [alloc(16 * 512 * 2, BF16, "p (j c) -> p j c", j=16) for _ in range(2)]
    qsets = [tuple(alloc(S * 2, BF16) for _ in range(3)) for _ in range(2)]
    V = alloc(NT * 256 * 2, BF16, "p (t c) -> p t c", t=NT)
    VA = [alloc(NT * 130 * 2, BF16, "p (t c) -> p t c", t=NT) for _ in range(2)]
    gate_ret = alloc(NT * 256 * 2, BF16, "p (t c) -> p t c", t=NT)
    gate_diff = [alloc(NT * 128 * 2, BF16, "p (t c) -> p t c", t=NT) for _ in range(2)]
    hkb1 = alloc(640 * 2, BF16)
    ehk = [alloc(640 * 2, BF16) for _ in range(2)]
    tA = [alloc(512 * 4, F32) for _ in range(2)]
    tB_base = cur[0]
    tB = [alloc(512 * 4, F32) for _ in range(2)]
    praw = [alloc(512 * 2, BF16, base=tB_base + 2048 * i) for i in range(2)]
    sqb = [alloc(512 * 2, BF16)] * 2
    tP = [alloc(128 * 4, F32) for _ in range(2)]
    osb = [alloc(128 * 4, F32) for _ in range(2)]
    tmpb = [alloc(128 * 2, BF16) for _ in range(2)]
    jk = [alloc(256 * 2, BF16) for _ in range(2)]
    P1_END = cur[0]
    xs0 = [alloc(D * 4, F32, base=merged_base + i * 4096) for i in range(2)]
    xh0 = [alloc(D * 2, BF16, base=merged_base + 8192 + i * 2048) for i in range(2)]
    cur[0] = U0
    WD = [alloc(NFC * 512 * 2, BF16, "p (f c) -> p f c", f=NFC) for _ in range(2)]
    actT = alloc(NFC * 512 * 2, BF16, "p (f c) -> p f c", f=NFC)
    sgb = [alloc(512 * 4, F32) for _ in range(2)]
    xs2 = [alloc(512 * 4, F32) for _ in range(4)]
    yst = [alloc(512 * 4, F32) for _ in range(2)]
    xh2 = [alloc(D * 2, BF16) for _ in range(2)]
    junk = alloc(D * 2, BF16)
    P2_END = cur[0]
    x1 = alloc(4 * D * 4, F32, "p (t c) -> p t c", t=4, base=merged_base)
    h2T = alloc(8 * 512 * 2, BF16, "p (k n) -> p k n", k=8, base=merged_base + 16384)
    print("phase1 end", P1_END, "phase2 end", P2_END, "limit", TOT)
    assert P1_END <= TOT and P2_END <= TOT

    banks = [nc.alloc_psum_tensor("pb%d" % i, [128, 512], F32) for i in range(8)]

    def bk(i):
        return banks[i][:, :]

    def bk16(i):
        return banks[i][:, :].bitcast(BF16)

    add = sch.add

    def dma(eng, out, in_, r, w, key):
        add(eng, lambda e, o=out, i=in_: e.dma_start(out=o, in_=i), r=r, w=w, dma_key=key)

    dma("sp", ident, ident_d, [], [("c_ident",)], "c0")
    dma("sp", antii, antii_d, [], [("c_antii",)], "c0")
    dma("sp", blk, blk_d, [], [("c_blk",)], "c0")
    dma("sp", cs, cs_d, [], [("c_cs",)], "c0")
    dma("sp", tabb, tabb_d, [], [("c_tabb",)], "c0")
    dma("sp", g1t, g1t_d, [], [("c_g1t",)], "c0")
    dma("sp", g2t, g2t_d, [], [("c_g2t",)], "c0")
    dma("sp", sg8, subln_d, [], [("c_sg8",)], "c0")
    cur[0] = U0
    cf = alloc(1426 * 4, F32)
    t1s = alloc(640 * 4, F32)
    t2s = alloc(640 * 4, F32)
    lamv = alloc(256 * 4, F32, "p (a c) -> p a c", a=4)
    prod = alloc(128 * 4, F32, "p (a c) -> p a c", a=2)
    decs = alloc(8 * 4, F32)
    gqk = alloc(2 * 4, F32)
    m01 = alloc(2 * 4, F32)
    tab32 = alloc(8 * 4, F32)
    oh1 = alloc(768 * 4, F32)
    gvsb = alloc(768 * 4, F32)
    stg = [alloc(8 * 512 * 2, BF16, "p (k c) -> p k c", k=8) for _ in range(4)]
    stgf = [alloc(8 * 512 * 4, F32, "p (k c) -> p k c", k=8) for _ in range(3)]
    assert cur[0] <= TOT
    dma("sp", cf, cf32_d, [], [("s_cf",)], "c1")
    dma("sp", lamv, lamv_d, [], [("s_lamv",)], "c1")
    dma("sp", decs, dec_d, [], [("s_dec",)], "c1")
    dma("sp", gqk, gqk_d, [], [("s_gqk",)], "c1")
    dma("sp", m01, m01_d, [], [("s_m01",)], "c1")
    dma("sp", tab32[0:32, :], tab_d, [], [("s_tab",)], "c1")
    dma("sp", oh1[0:32, 0:767], oh1_d, [], [("s_oh1",)], "c1")

    add("act", lambda e: e.activation(out=stat[:, 0:8], in_=decs, func=AF.Exp), r=[("s_dec",)], w=[("st", 0)])
    add("dve", lambda e: e.tensor_scalar(out=stat[:, 8:16], in0=stat[:, 0:8], scalar1=-1.0, scalar2=1.0,
                                         op0=ALU.mult, op1=ALU.add), r=[("st", 0)], w=[("st", 1)])
    add("act", lambda e: e.activation(out=lg[:, 0:8], in_=stat[:, 8:16], func=AF.Ln), r=[("st", 1)], w=[("c_lg", 0)])
    add("dve", lambda e: e.tensor_scalar(out=lg[:, 8:16], in0=lg[:, 0:8], scalar1=-1.0, scalar2=None, op0=ALU.mult),
        r=[("c_lg", 0)], w=[("c_lg", 1)])
    lnsc = math.log(128.0 ** -0.5)
    add("act", lambda e: e.activation(out=cdc, in_=lg[:, 0:8], func=AF.Exp, scale=128.0), r=[("c_lg", 0)], w=[("c_cdc",)])
    for idx in range(8):
        add("act", lambda e, idx=idx: e.activation(out=pws[:, idx, :], in_=cf[:, 0:16], func=AF.Exp,
                                                    scale=lg[:, idx:idx + 1], bias=lnsc),
            r=[("s_cf",), ("c_lg", 0)], w=[("c_pws", idx)])
    for r_ in range(4):
        add("act", lambda e, r_=r_: e.activation(out=Xi[:, r_, :], in_=cf[:, 1170:1298], func=AF.Exp, scale=lg[:, r_:r_ + 1]),
            r=[("s_cf",), ("c_lg", 0)], w=[("c_xi", r_)])
        add("act", lambda e, r_=r_: e.activation(out=Xi[:, 4 + r_, :], in_=cf[:, 1298:1426], func=AF.Exp, scale=lg[:, 4 + r_:5 + r_]),
            r=[("s_cf",), ("c_lg", 0)], w=[("c_xi", 4 + r_)])
        add("act", lambda e, r_=r_: e.activation(out=zt[:, r_:r_ + 1], in_=cf[:, 1168:1169], func=AF.Exp, scale=lg[:, r_:r_ + 1], bias=lnsc),
            r=[("s_cf",), ("c_lg", 0)], w=[("c_zt", r_)])
        add("act", lambda e, r_=r_: e.activation(out=zt[:, 4 + r_:5 + r_], in_=cf[:, 1169:1170], func=AF.Exp, scale=lg[:, 4 + r_:5 + r_], bias=lnsc),
            r=[("s_cf",), ("c_lg", 0)], w=[("c_zt", 4 + r_)])
        add("dve", lambda e, r_=r_: e.tensor_scalar(out=t1s, in0=cf[:, 528:1168], scalar1=lg[:, 8 + r_:9 + r_],
                                                     scalar2=None, op0=ALU.mult),
            r=[("s_cf",), ("c_lg", 1)], w=[("s_t1",)])
        add("dve", lambda e, r_=r_: e.scalar_tensor_tensor(out=t2s, in0=cf[:, 528:1168], scalar=lg[:, 4 + r_:5 + r_],
                                                            in1=t1s, op0=ALU.mult, op1=ALU.min),
            r=[("s_cf",), ("c_lg", 0), ("s_t1",)], w=[("s_t2",)])
        add("act", lambda e, r_=r_: e.activation(out=dn[:, r_, :], in_=t2s, func=AF.Exp),
            r=[("s_t2",)], w=[("c_dn", r_)])
    add("dve", lambda e: e.tensor_tensor(out=prod, in0=lamv[:, 0:2, :], in1=lamv[:, 2:4, :], op=ALU.mult),
        r=[("s_lamv",)], w=[("s_prod",)])
    add("dve", lambda e: e.reduce_sum(out=lamc[:, 0:2], in_=prod, axis=AX.X), r=[("s_prod",)], w=[("c_lam", 0)])
    add("act", lambda e: e.activation(out=lamc[:, 2:4], in_=lamc[:, 0:2], func=AF.Exp), r=[("c_lam", 0)], w=[("c_lam", 1)])
    add("dve", lambda e: e.tensor_tensor(out=lamc[:, 4:5], in0=lamc[:, 3:4], in1=lamc[:, 2:3], op=ALU.subtract),
        r=[("c_lam", 1)], w=[("c_lam", 2)])
    add("dve", lambda e: e.tensor_scalar(out=lamc[:, 5:6], in0=lamc[:, 4:5], scalar1=-LAMBDA_INIT, scalar2=None, op0=ALU.add),
        r=[("c_lam", 2)], w=[("c_nlam",)])
    NLAM = lamc[:, 5:6]
    add("dve", lambda e: e.tensor_scalar(out=gcols[:, 0:1], in0=gqk[:, 0:1], scalar1=0.125, scalar2=None, op0=ALU.mult),
        r=[("s_gqk",)], w=[("c_gc", 0)])
    add("dve", lambda e: e.tensor_tensor(out=gcols[:, 1:3], in0=m01, in1=gqk[:, 1:2].to_broadcast([128, 2]), op=ALU.mult),
        r=[("s_gqk",), ("s_m01",)], w=[("c_gc", 1)])
    add("dve", lambda e: e.tensor_scalar(out=sg8, in0=sg8, scalar1=1.0 - LAMBDA_INIT, scalar2=None, op0=ALU.mult),
        r=[("c_sg8",)], w=[("c_sg8",)])
    add("pe", lambda e: (e.matmul(banks[0][0:8, 0:512], lhsT=tab32[0:32, 0:8], rhs=oh1[0:32, 0:512], start=True, stop=True),
                         e.matmul(banks[1][0:8, 0:255], lhsT=tab32[0:32, 0:8], rhs=oh1[0:32, 512:767], start=True, stop=True))[1],
        r=[("s_tab",), ("s_oh1",)], w=[("ps", 0), ("ps", 1)])
    add("dve", lambda e: e.tensor_copy(gvsb[0:8, 0:512], banks[0][0:8, 0:512]), r=[("ps", 0)], w=[("s_gv", 0)])
    add("dve", lambda e: e.tensor_copy(gvsb[0:8, 512:767], banks[1][0:8, 0:255]), r=[("ps", 1)], w=[("s_gv", 1)])
    dma("sp", gv_s, gvsb[0:8, 0:767], [("s_gv",)], [("d_gv",)], "c2")

    stg_rot = Rot(list(range(4)))
    stgf_rot = Rot(list(range(3)))
    cv_rot = Rot(["dve", "pool", "dve"])
    wscr_n = [0]

    def cast_group(dst, pieces, nk=8):
        sl = stg_rot.next()
        sf = stgf_rot.next()
        for (src, off, n) in pieces:
            dma("sp", stgf[sf][:, 0:nk, off:off + n], src, [], [("stgf", sf)], "stgf%d" % sf)
        ncol = max(off + n for (_, off, n) in pieces)
        add(cv_rot.next(), lambda e: e.tensor_copy(stg[sl][:, 0:nk, 0:ncol], stgf[sf][:, 0:nk, 0:ncol]),
            r=[("stgf", sf)], w=[("stg", sl)])
        wscr_n[0] += 1
        dma("act", dst, stg[sl][:, 0:nk, :], [("stg", sl)], [("wscr", wscr_n[0])], "stgo%d" % sl)

    def wcols(wd, c0, n):
        return wd[:, c0:c0 + n].rearrange("(k p) c -> p k c", p=128)

    for cg in range(4):
        r_ = cg
        q0 = C_RQ + 128 * r_
        k0 = C_RK + 128 * r_
        g0 = [(wcols(w_in_d, q0, 128), 0, 128),
              (wcols(w_in_d, q0 + 64, 64), 128, 64), (wcols(w_in_d, q0, 64), 192, 64),
              (wcols(w_in_d, k0, 128), 256, 128),
              (wcols(w_in_d, k0 + 64, 64), 384, 64), (wcols(w_in_d, k0, 64), 448, 64)]
        g1 = [(wcols(w_in_d, C_RV + 256 * r_, 256), 0, 256), (wcols(w_in_d, C_RG + 256 * r_, 256), 256, 256)]
        g2 = [(wcols(w_in_d, C_MGR + 256 * r_, 256), 0, 256), (wcols(w_in_d, C_MGR + 256 * r_, 256), 256, 256)]
        grp = [g0, g1, g2]
        for hh in range(2):
            h = 2 * cg + hh
            grp.append([(wcols(w_in_d, C_DQ + 128 * h, 128), 0, 128), (wcols(w_in_d, C_DK + 128 * h, 128), 128, 128),
                        (wcols(w_in_d, C_DV + 128 * h, 128), 256, 128), (wcols(w_in_d, C_MGD + 128 * h, 128), 384, 128)])
        for gi, pieces in enumerate(grp):
            cast_group(win_s[cg * 5 + gi].rearrange("(k p) c -> p k c", p=128), pieces)
    for c2 in range(2):
        cast_group(wo_s[c2].rearrange("(k p) c -> p k c", p=128), [(wcols(w_out_d, 512 * c2, 512), 0, 512)])
    for g in range(11):
        cast_group(wgu_s[g].rearrange("(k p) c -> p k c", p=128),
                   [(wcols(w_gate_d, 256 * g, 256), 0, 256), (wcols(w_up_d, 256 * g, 256), 256, 256)])
    for c2 in range(2):
        for (f0, nf) in ((0, 8), (8, 8), (16, 6)):
            src = w_down_d[f0 * 128:(f0 + nf) * 128, 512 * c2:512 * c2 + 512].rearrange("(k p) c -> p k c", p=128)
            dst = wd_s[c2][f0 * 128:(f0 + nf) * 128, :].rearrange("(k p) c -> p k c", p=128)
            cast_group(dst, [(src, 0, 512)], nk=nf)
    sch.barrier()

    def w_src(tag):
        kind, i = tag
        t = {"in": win_s, "wo": wo_s, "gu": wgu_s}[kind]
        return t[i].rearrange("(k p) c -> p k c", p=128)

    seq_wlist = []
    for cg_ in range(4):
        seq_wlist += [("in", cg_ * 5 + g_) for g_ in range(5)]
    for qd_ in range(4):
        seq_wlist += [("wo", 0), ("wo", 1)] + [("gu", g_) for g_ in range(11)]

    class WStream:
        def __init__(self, tags):
            self.tags = tags
            self.pos = 0
            self.issued = 0
            self.slot_of = {}

        def _issue(self):
            if self.issued < len(self.tags):
                sl = self.issued % 2
                dma("sp", wring[sl], w_src(self.tags[self.issued]), [], [("W", sl)], "w%d" % sl)
                self.slot_of[self.issued] = sl
                self.issued += 1

        def take(self, tag):
            while self.tags[self.pos] != tag:
                self.pos += 1
            if self.issued < self.pos:
                self.issued = self.pos
            while self.issued <= self.pos and self.issued < len(self.tags):
                self._issue()
            sl = self.slot_of[self.pos]
            self.pos += 1
            return sl

        def prefetch(self):
            while self.issued <= self.pos and self.issued < len(self.tags):
                self._issue()

    wstream = WStream(seq_wlist * nseq)

    for s in range(nseq):
        tb_rot = Rot([0, 1])
        for tt in range(NT):
            sl = tt % 2
            dma("sp", xs0[sl], x_d[s, tt * 128:(tt + 1) * 128, :], [], [("xs0", sl)], "xs0_%d" % sl)
            st = stat[:, 16 + 4 * sl:20 + 4 * sl]
            add("act", lambda e, sl=sl, st=st: e.activation(out=xh0[sl], in_=xs0[sl], func=AF.Square, accum_out=st[:, 0:1]),
                r=[("xs0", sl)], w=[("xh0", sl), ("st0", sl, 0)])
            add("act", lambda e, st=st: e.activation(out=st[:, 1:2], in_=st[:, 0:1], func=AF.Ln, scale=1.0 / D, bias=EPS),
                r=[("st0", sl, 0)], w=[("st0", sl, 1)])
            add("act", lambda e, st=st: e.activation(out=st[:, 2:3], in_=st[:, 1:2], func=AF.Exp, scale=-0.5),
                r=[("st0", sl, 1)], w=[("st0", sl, 2)])
            add("act", lambda e, sl=sl, st=st: e.activation(out=xh0[sl], in_=xs0[sl], func=AF.Copy, scale=st[:, 2:3]),
                r=[("xs0", sl), ("st0", sl, 2)], w=[("xh0", sl)])
            b = tb_rot.next()
            pt = bk16(b).rearrange("p (k n) -> p k n", k=8)

            def tr(e, sl=sl, pt=pt):
                ins = None
                for kc in range(8):
                    ins = e.transpose(out=pt[:, kc, :], in_=xh0[sl][:, kc * 128:(kc + 1) * 128], identity=ident)
                return ins
            add("pe", tr, r=[("xh0", sl), ("c_ident",)], w=[("ps", b)])
            add("dve", lambda e, pt=pt, tt=tt: e.tensor_tensor(out=hT[:, :, tt * 128:(tt + 1) * 128], in0=pt,
                                                                in1=g1t.unsqueeze(2).to_broadcast([128, 8, 128]), op=ALU.mult),
                r=[("ps", b), ("c_g1t",)], w=[("hT", tt)])
        sch.barrier()

        pj = Rot([0, 1, 4])
        sb = Rot([2, 3])
        ab = Rot([5, 6, 7])
        eslot = Rot([0, 1])
        tmp_rot = Rot([0, 1])
        pp_rot = Rot([0, 1])
        jk_rot = Rot([0, 1])

        def hT_keys(tb):
            return [("hT", 4 * tb + i) for i in range(4)]

        def proj_fm(wsl, c0, tb, b, m=128):
            def f(e):
                ins = None
                for kc in range(8):
                    ins = e.matmul(bk(b)[0:m, :], lhsT=wring[wsl][:, kc, c0:c0 + m], rhs=hT[:, kc, tb * 512:(tb + 1) * 512],
                                   start=(kc == 0), stop=(kc == 7))
                return ins
            add("pe", f, r=[("W", wsl)] + hT_keys(tb), w=[("ps", b)])

        def proj_tm(wsl, c0, n, tt, b):
            def f(e):
                ins = None
                for kc in range(8):
                    ins = e.matmul(bk(b)[:, 0:n], lhsT=hT[:, kc, tt * 128:(tt + 1) * 128], rhs=wring[wsl][:, kc, c0:c0 + n],
                                   start=(kc == 0), stop=(kc == 7))
                return ins
            add("pe", f, r=[("W", wsl), ("hT", tt)], w=[("ps", b)])

        def g_proj_fm(wsl, c0, tb, b):
            for half in range(2):
                def f(e, half=half):
                    ins = None
                    for kc in range(4 * half, 4 * half + 4):
                        ins = e.matmul(bk(b), lhsT=wring[wsl][:, kc, c0:c0 + 128], rhs=hT[:, kc, tb * 512:(tb + 1) * 512],
                                       start=(kc == 0), stop=(kc == 7))
                    return ins
                add("pe", f, r=[("W", wsl)] + hT_keys(tb), w=[("ps", b)])
                yield

        def g_proj_tm(wsl, c0, n, tt, b):
            for half in range(2):
                def f(e, half=half):
                    ins = None
                    for kc in range(4 * half, 4 * half + 4):
                        ins = e.matmul(bk(b)[:, 0:n], lhsT=hT[:, kc, tt * 128:(tt + 1) * 128], rhs=wring[wsl][:, kc, c0:c0 + n],
                                       start=(kc == 0), stop=(kc == 7))
                    return ins
                add("pe", f, r=[("W", wsl), ("hT", tt)], w=[("ps", b)])
                yield

        def proj_units_R(cg, st_):
            QT, K0T, K1T = qsets[st_]
            units = []
            wh = {}

            def getw(name, gi):
                if name not in wh:
                    wh[name] = wstream.take(("in", cg * 5 + gi))
                    wstream.prefetch()
                return wh[name]

            def u_rot(tb, c0, dst, dkey):
                def u():
                    wsl = getw("g0", 0)
                    tok = slice(tb * 512, (tb + 1) * 512)
                    b1 = pj.next()
                    yield from g_proj_fm(wsl, c0, tb, b1)
                    yield
                    ti = tmp_rot.next()
                    add("dve", lambda e: e.tensor_tensor(out=tA[ti], in0=bk(b1), in1=cs[:, 0, tok], op=ALU.mult),
                        r=[("ps", b1), ("c_cs",)], w=[("tA", ti)])
                    b2 = pj.next()
                    yield from g_proj_fm(wsl, c0 + 128, tb, b2)
                    yield
                    add("dve", lambda e: e.tensor_tensor(out=tB[ti], in0=bk(b2), in1=cs[:, 1, tok], op=ALU.mult),
                        r=[("ps", b2), ("c_cs",)], w=[("tB", ti)])
                    yield
                    add("pool", lambda e: e.tensor_tensor(out=dst[:, tok], in0=tA[ti], in1=tB[ti], op=ALU.add),
                        r=[("tA", ti), ("tB", ti)], w=[(dkey, st_, tb)])
                return u
            for tb in range(4):
                units.append((u_rot(tb, 0, QT, "QT"), 6))
                units.append((u_rot(tb, 256, K0T, "K0T"), 6))

            def u_v(tt):
                def u():
                    wsl = getw("g1", 1)
                    b = pj.next()
                    yield from g_proj_tm(wsl, 0, 512, tt, b)
                    yield
                    ti = tmp_rot.next()
                    add("act", lambda e: e.activation(out=tA[ti][:, 0:256], in_=bk(b)[:, 256:512], func=AF.Tanh, scale=0.5),
                        r=[("ps", b)], w=[("tA", ti)])
                    yield
                    add("dve", lambda e: e.tensor_copy(V[:, tt, :], bk(b)[:, 0:256]), r=[("ps", b), ("tA", ti)], w=[("V", tt)])
                    add("dve", lambda e: e.scalar_tensor_tensor(out=gate_ret[:, tt, :], in0=tA[ti][:, 0:256], scalar=1.0,
                                                                 in1=bk(b)[:, 256:512], op0=ALU.add, op1=ALU.mult),
                        r=[("ps", b), ("tA", ti)], w=[("gr", tt)])
                return u
            for tt in range(NT):
                units.append((u_v(tt), 3))

            def u_g(tt):
                def u():
                    wsl = getw("g2", 2)
                    b = pj.next()
                    yield from g_proj_tm(wsl, 0, 256, tt, b)
                    yield
                    ti = tmp_rot.next()
                    add("act", lambda e: e.activation(out=tA[ti][:, 0:256], in_=bk(b)[:, 0:256], func=AF.Tanh, scale=0.5),
                        r=[("ps", b)], w=[("tA", ti)])
                    yield
                    add("dve", lambda e: e.scalar_tensor_tensor(out=gate_ret[:, tt, :], in0=tA[ti][:, 0:256], scalar=1.0,
                                                                 in1=gate_ret[:, tt, :], op0=ALU.add, op1=ALU.mult),
                        r=[("tA", ti), ("gr", tt)], w=[("gr", tt)])
                return u
            for tt in range(NT):
                units.append((u_g(tt), 3))
            return units

        def proj_units_D(h, st_, sl_):
            QT, K0T, K1T = qsets[st_]
            cg = h // 2
            units = []
            wh = {}

            def getw():
                if "w" not in wh:
                    wh["w"] = wstream.take(("in", cg * 5 + 3 + (h % 2)))
                    wstream.prefetch()
                    add("pool", lambda e: e.memset(VA[sl_][:, :, 128:129], 1.0), r=[], w=[("VAone", sl_)])
                return wh["w"]

            def u_qk(tb, which):
                def u():
                    wsl = getw()
                    tok = slice(tb * 512, (tb + 1) * 512)
                    b1 = pj.next()
                    yield from g_proj_fm(wsl, 128 * which, tb, b1)
                    yield
                    ti = tmp_rot.next()
                    add("act", lambda e: e.activation(out=sqb[ti], in_=bk(b1), func=AF.Square), r=[("ps", b1)], w=[("sqb", ti)])
                    yield
                    add("dve", lambda e: e.tensor_copy(praw[ti], bk(b1)), r=[("ps", b1), ("sqb", ti)], w=[("tB", ti)])
                    yield
                    b2 = pj.next()
                    add("pe", lambda e: e.matmul(bk(b2), lhsT=blk, rhs=sqb[ti], start=True, stop=True),
                        r=[("sqb", ti), ("c_blk",)], w=[("ps", b2)])
                    yield
                    yield
                    add("act", lambda e: e.activation(out=tA[ti], in_=bk(b2), func=AF.Ln, bias=EPS), r=[("ps", b2)], w=[("tA", ti)])
                    yield
                    yield
                    add("act", lambda e: e.activation(out=tA[ti], in_=tA[ti], func=AF.Exp, scale=-0.5), r=[("tA", ti)], w=[("tA", ti)])
                    yield
                    yield
                    if which == 0:
                        add("dve", lambda e: e.scalar_tensor_tensor(out=QT[:, tok], in0=praw[ti], scalar=gcols[:, 0:1], in1=tA[ti],
                                                                     op0=ALU.mult, op1=ALU.mult),
                            r=[("tB", ti), ("tA", ti), ("c_gc",)], w=[("QT", st_, tb)])
                    else:
                        add("dve", lambda e: e.scalar_tensor_tensor(out=K0T[:, tok], in0=praw[ti], scalar=gcols[:, 1:2], in1=tA[ti],
                                                                     op0=ALU.mult, op1=ALU.mult),
                            r=[("tB", ti), ("tA", ti), ("c_gc",)], w=[("K0T", st_, tb)])
                        add("dve", lambda e: e.scalar_tensor_tensor(out=K1T[:, tok], in0=praw[ti], scalar=gcols[:, 2:3], in1=tA[ti],
                                                                     op0=ALU.mult, op1=ALU.mult),
                            r=[("tB", ti), ("tA", ti), ("c_gc",)], w=[("K1T", st_, tb)])
                return u
            for tb in range(4):
                units.append((u_qk(tb, 0), 10, 5))
                units.append((u_qk(tb, 1), 10, 5))

            def u_v(tt):
                def u():
                    wsl = getw()
                    b = pj.next()
                    yield from g_proj_tm(wsl, 256, 256, tt, b)
                    yield
                    ti = tmp_rot.next()
                    add("act", lambda e: e.activation(out=tA[ti][:, 0:128], in_=bk(b)[:, 128:256], func=AF.Tanh, scale=0.5),
                        r=[("ps", b)], w=[("tA", ti)])
                    yield
                    add("dve", lambda e: e.tensor_copy(VA[sl_][:, tt, 0:128], bk(b)[:, 0:128]), r=[("ps", b), ("tA", ti)], w=[("VA", sl_, tt)])
                    add("dve", lambda e: e.scalar_tensor_tensor(out=gate_diff[sl_][:, tt, :], in0=tA[ti][:, 0:128], scalar=1.0,
                                                                 in1=sg8[:, 0:128], op0=ALU.add, op1=ALU.mult),
                        r=[("tA", ti), ("c_sg8",)], w=[("gd", sl_, tt)])
                return u
            for tt in range(NT):
                units.append((u_v(tt), 4, 3))
            return units

        def attn_units_R(r_, st_, spawn):
            QT, K0T, K1T = qsets[st_]
            qfT = K1T
            units = []

            def SFb(c):
                return E[0][:, c, 0:256]

            def SBb(c):
                return E[0][:, c, 256:512]

            def prep(g):
                def u():
                    if g == 0:
                        add("pool", lambda e: e.memset(SFb(0), 0.0), r=[], w=[("E", 0, 0, "f")])
                        add("pool", lambda e: e.memset(SBb(15), 0.0), r=[], w=[("E", 0, 15, "b")])
                    b = sb.next()
                    pt = bk16(b)[:, 0:512].rearrange("p (k n) -> p k n", k=4)

                    def tr(e):
                        ins = None
                        for k in range(4):
                            c = 4 * g + k
                            ins = e.transpose(out=pt[:, k, :], in_=K0T[:, c * 128:(c + 1) * 128], identity=ident)
                        return ins
                    add("pe", tr, r=[("K0T", st_, g), ("c_ident",)], w=[("ps", b)])
                    add("act", lambda e: e.activation(out=E[1][:, 4 * g:4 * g + 4, 128:256], in_=pt, func=AF.Copy, scale=zt[:, r_:r_ + 1]),
                        r=[("ps", b), ("c_zt", r_)], w=[("E", 1, 4 * g + k, "kf") for k in range(4)])
                    add("act", lambda e: e.activation(out=E[1][:, 4 * g:4 * g + 4, 256:384], in_=pt, func=AF.Copy, scale=zt[:, 4 + r_:5 + r_]),
                        r=[("ps", b), ("c_zt", 4 + r_)], w=[("E", 1, 4 * g + k, "kb") for k in range(4)])
                    qv = QT[:, g * 512:(g + 1) * 512].rearrange("p (k n) -> p k n", k=4)
                    add("dve", lambda e: e.tensor_tensor(out=qfT[:, g * 512:(g + 1) * 512].rearrange("p (k n) -> p k n", k=4), in0=qv,
                                                          in1=Xi[:, r_, :].unsqueeze(1).to_broadcast([128, 4, 128]), op=ALU.mult),
                        r=[("QT", st_, g), ("c_xi", r_)], w=[("K1T", st_, g)])
                    add("dve", lambda e: e.tensor_tensor(out=E[1][:, 4 * g:4 * g + 4, 384:512], in0=qv,
                                                          in1=Xi[:, 4 + r_, :].unsqueeze(1).to_broadcast([128, 4, 128]), op=ALU.mult),
                        r=[("QT", st_, g), ("c_xi", 4 + r_)], w=[("E", 1, 4 * g + k, "qb") for k in range(4)])
                return u

            def hk_load(hh):
                def u():
                    src = bass.AP(gvb_t, 767 * (2 * r_ + hh), [[1, 128], [1, 640]])
                    dma("pool", hkb1, src, [], [("hkb",)], "hk0")
                return u

            def hk_flip(hh):
                def u():
                    for (c0, n) in ((0, 512), (512, 128)):
                        b = sb.next()
                        add("pe", lambda e, b=b, c0=c0, n=n: e.matmul(bk(b)[:, 0:n], lhsT=antii, rhs=hkb1[:, c0:c0 + n], start=True, stop=True),
                            r=[("hkb",), ("c_antii",)], w=[("ps", b)])
                        add("act", lambda e, b=b, c0=c0, n=n: e.activation(out=ehk[hh][:, c0:c0 + n], in_=bk(b)[:, 0:n], func=AF.Exp),
                            r=[("ps", b)], w=[("ehk", hh, c0)])
                return u

            def scan(c):
                def u():
                    cb = 15 - c
                    bu = ab.next()

                    def um(e):
                        ins = None
                        if c <= 14:
                            ins = e.matmul(bk(bu)[:, 0:256], lhsT=E[1][:, c, 128:256], rhs=V[:, c, :], start=True, stop=True)
                        if cb >= 1:
                            ins = e.matmul(bk(bu)[:, 256:512], lhsT=E[1][:, cb, 256:384], rhs=V[:, cb, :], start=(c > 14), stop=True)
                        return ins
                    if c <= 14 or cb >= 1:
                        add("pe", um, r=[("E", 1, c, "kf"), ("E", 1, cb, "kb"), ("V", c), ("V", cb)], w=[("ps", bu)])
                    bs = sb.next()
                    add("pe", lambda e: e.matmul(bk(bs)[:, 0:128], lhsT=K0T[:, c * 128:(c + 1) * 128], rhs=QT[:, c * 128:(c + 1) * 128],
                                                 start=True, stop=True),
                        r=[("K0T", st_, c // 4), ("QT", st_, c // 4)], w=[("ps", bs)])
                    add("dve", lambda e: e.scalar_tensor_tensor(out=E[1][:, c, 0:128], in0=bk(bs)[:, 0:128], scalar=pws[:, r_, 0:1],
                                                                 in1=dn[:, r_, 256:384], op0=ALU.mult, op1=ALU.mult),
                        r=[("ps", bs), ("c_dn", r_), ("c_pws",)], w=[("E", 1, c, "p")])
                    if c <= 14:
                        add("dve", lambda e: e.scalar_tensor_tensor(out=SFb(c + 1), in0=SFb(c), scalar=cdc[:, r_:r_ + 1], in1=bk(bu)[:, 0:256],
                                                                     op0=ALU.mult, op1=ALU.add),
                            r=[("ps", bu), ("E", 0, c, "f"), ("c_cdc",)], w=[("E", 0, c + 1, "f")])
                    if cb >= 1:
                        add("dve", lambda e: e.scalar_tensor_tensor(out=SBb(cb - 1), in0=SBb(cb), scalar=cdc[:, 4 + r_:5 + r_],
                                                                     in1=bk(bu)[:, 256:512], op0=ALU.mult, op1=ALU.add),
                            r=[("ps", bu), ("E", 0, cb, "b"), ("c_cdc",)], w=[("E", 0, cb - 1, "b")])
                return u

            def post(b, tt):
                ti = pp_rot.next()
                ji = jk_rot.next()
                st = stat[:, 24 + 4 * ti:28 + 4 * ti]
                yield
                add("act", lambda e: e.activation(out=jk[ji], in_=bk(b)[:, 0:256], func=AF.Square, accum_out=st[:, 0:1]),
                    r=[("ps", b)], w=[("jk", ji), ("st1", ti, 0)])
                yield
                yield
                add("act", lambda e: e.activation(out=st[:, 1:2], in_=st[:, 0:1], func=AF.Ln, scale=1.0 / 256, bias=EPS),
                    r=[("st1", ti, 0)], w=[("st1", ti, 1)])
                yield
                add("act", lambda e: e.activation(out=st[:, 2:3], in_=st[:, 1:2], func=AF.Exp, scale=-0.5, bias=math.log(0.25)),
                    r=[("st1", ti, 1)], w=[("st1", ti, 2)])
                yield
                add("dve", lambda e: e.scalar_tensor_tensor(out=merged[:, tt, 256 * r_:256 * r_ + 256], in0=bk(b)[:, 0:256],
                                                             scalar=st[:, 2:3], in1=gate_ret[:, tt, :], op0=ALU.mult, op1=ALU.mult),
                    r=[("ps", b), ("st1", ti, 2), ("gr", tt)], w=[("mg", tt, r_)])

            def outc(c):
                def u():
                    b = ab.next()

                    def om(e):
                        e.matmul(bk(b)[:, 0:256], lhsT=E[1][:, c, 0:128], rhs=V[:, c, :], start=True, stop=False)
                        e.matmul(bk(b)[:, 0:256], lhsT=qfT[:, c * 128:(c + 1) * 128], rhs=SFb(c), start=False, stop=False)
                        return e.matmul(bk(b)[:, 0:256], lhsT=E[1][:, c, 384:512], rhs=SBb(c), start=False, stop=True)
                    add("pe", om, r=[("E", 1, c, "p"), ("E", 1, c, "qb"), ("K1T", st_, c // 4), ("E", 0, c, "f"), ("E", 0, c, "b"), ("V", c)],
                        w=[("ps", b)])
                    if os.environ.get("KPOST", "1") == "1":
                        spawn(post(b, c))
                return u

            krs = int(os.environ.get("KRS", "2"))
            units.append(hk_load(0))
            for g in range(4):
                units.append(prep(g))
            if krs >= 1:
                for c in range(NT):
                    units.append(scan(c))
                    if c == 7:
                        units.append(hk_flip(0))
                        units.append(hk_load(1))
            units.append(hk_flip(1))
            for _ in range(6):
                units.append(lambda: None)
            if krs >= 2:
                for c in range(NT - 1, -1, -1):
                    units.append(outc(c))
                    units.append(lambda: None)
            while len(units) < 150:
                units.append(lambda: None)
            return units

        def attn_units_D(h, st_, sl_, spawn):
            QT, K0T, K1T = qsets[st_]
            cg = h // 2
            units = []
            state = {}

            def qk_step(ib, jt):
                es = state[("es", ib)]
                it0 = 2 * ib
                b = sb.next()
                dlt = jt - it0
                near = -1 <= dlt <= 2
                if near:
                    c0 = 256 - 128 * dlt
                    hkw = ehk[h % 2][:, c0:c0 + 256]
                    biasarg = 0.0
                    bkey = ("ehk", h % 2)
                else:
                    col = (15 if dlt < 0 else 31) * 8 + h
                    biasarg = tabb[:, col:col + 1]
                    bkey = ("c_tabb",)
                    hk2 = None

                def qk(e):
                    e.matmul(bk(b)[:, 0:256], lhsT=K0T[:, jt * 128:(jt + 1) * 128], rhs=QT[:, ib * 256:(ib + 1) * 256],
                             start=True, stop=False)
                    return e.matmul(bk(b)[:, 256:512], lhsT=K1T[:, jt * 128:(jt + 1) * 128], rhs=QT[:, ib * 256:(ib + 1) * 256],
                                    start=False, stop=True)
                add("pe", qk, r=[("K0T", st_, jt // 4), ("K1T", st_, jt // 4), ("QT", st_, ib // 2)], w=[("ps", b)])
                if near:
                    add("act", lambda e: e.activation(out=E[es][:, jt, :], in_=bk(b), func=AF.Exp), r=[("ps", b)], w=[("E", es, jt)])

                    def mulw(e):
                        e.tensor_tensor(out=E[es][:, jt, 0:256], in0=E[es][:, jt, 0:256], in1=hkw, op=ALU.mult)
                        return e.tensor_tensor(out=E[es][:, jt, 256:512], in0=E[es][:, jt, 256:512], in1=hkw, op=ALU.mult)
                    add("pool", mulw, r=[bkey, ("E", es, jt)], w=[("E", es, jt)])
                else:
                    add("act", lambda e: e.activation(out=E[es][:, jt, :], in_=bk(b), func=AF.Exp, bias=biasarg),
                        r=[("ps", b), bkey], w=[("E", es, jt)])

            def post(b, tt):
                ti = pp_rot.next()
                ji = jk_rot.next()
                st = stat[:, 32 + 8 * ti:40 + 8 * ti]
                rs_ap = bass.AP(bk(b).tensor, bk(b).offset + 128, [list(bk(b).ap[0]), [129, 2]])
                yield
                add("dve", lambda e: e.reciprocal(out=st[:, 0:2], in_=rs_ap), r=[("ps", b)], w=[("st2", ti, 0)])
                yield
                add("dve", lambda e: e.tensor_tensor(out=st[:, 2:3], in0=st[:, 1:2], in1=NLAM, op=ALU.mult),
                    r=[("st2", ti, 0), ("c_nlam",)], w=[("st2", ti, 1)])
                add("dve", lambda e: e.tensor_scalar(out=tP[ti], in0=bk(b)[:, 0:128], scalar1=st[:, 0:1], scalar2=None, op0=ALU.mult),
                    r=[("ps", b), ("st2", ti, 0)], w=[("tP", ti)])
                yield
                add("dve", lambda e: e.scalar_tensor_tensor(out=osb[ti], in0=bk(b)[:, 129:257], scalar=st[:, 2:3],
                                                             in1=tP[ti], op0=ALU.mult, op1=ALU.add),
                    r=[("ps", b), ("st2", ti, 1), ("tP", ti)], w=[("osb", ti)])
                yield
                add("dve", lambda e: e.scalar_tensor_tensor(out=jk[ji][:, 0:128], in0=osb[ti], scalar=1.0, in1=osb[ti],
                                                             op0=ALU.mult, op1=ALU.mult, accum_out=st[:, 3:4]),
                    r=[("osb", ti)], w=[("jk", ji), ("st2", ti, 3)])
                yield
                yield
                add("act", lambda e: e.activation(out=st[:, 4:5], in_=st[:, 3:4], func=AF.Ln, scale=1.0 / 128, bias=EPS),
                    r=[("st2", ti, 3)], w=[("st2", ti, 4)])
                yield
                yield
                add("act", lambda e: e.activation(out=st[:, 5:6], in_=st[:, 4:5], func=AF.Exp, scale=-0.5, bias=math.log(0.5)),
                    r=[("st2", ti, 4)], w=[("st2", ti, 5)])
                yield
                yield
                add("dve", lambda e: e.scalar_tensor_tensor(out=tmpb[ti], in0=osb[ti], scalar=st[:, 5:6],
                                                             in1=gate_diff[sl_][:, tt, :], op0=ALU.mult, op1=ALU.mult),
                    r=[("osb", ti), ("st2", ti, 5), ("gd", sl_, tt)], w=[("tmpb", ti)])
                yield
                mcol = 128 * h
                add("pool", lambda e: e.tensor_tensor(out=merged[:, tt, mcol:mcol + 128], in0=merged[:, tt, mcol:mcol + 128],
                                                       in1=tmpb[ti], op=ALU.add),
                    r=[("tmpb", ti), ("mg", tt, cg)], w=[("mg", tt, cg)])

            def pv_piece(ib, step):
                es = state[("es", ib)]
                g = step // 4
                i2, mp = g // 2, g % 2
                j0 = (step % 4) * 4
                if step % 8 == 0:
                    state[("ab", ib, i2)] = ab.next()
                b = state[("ab", ib, i2)]

                def f(e):
                    ins = None
                    for jt in range(j0, j0 + 4):
                        ins = e.matmul(bk(b)[:, mp * 129:mp * 129 + 129],
                                       lhsT=E[es][:, jt, mp * 256 + i2 * 128:mp * 256 + (i2 + 1) * 128],
                                       rhs=VA[sl_][:, jt, 0:129], start=(jt == 0), stop=(jt == NT - 1))
                    return ins
                add("pe", f, r=[("E", es), ("VA", sl_), ("VAone", sl_)], w=[("ps", b)])
                if step % 8 == 7:
                    spawn(post(b, 2 * ib + i2))

            def mk(ib, jt):
                def u():
                    if ib < 8:
                        if jt == 0:
                            state[("es", ib)] = eslot.next()
                        qk_step(ib, jt)
                    if ib > 0:
                        pv_piece(ib - 1, jt)
                return u
            for ib in range(9):
                for jt in range(NT):
                    units.append(mk(ib, jt))
            return units

        active = []

        def spawn(g):
            active.append(g)

        def advance():
            for g in list(active):
                try:
                    next(g)
                except StopIteration:
                    active.remove(g)

        tasks = []
        dcount = 0
        for cg in range(4):
            tasks.append(("R", cg))
            tasks.append(("D", 2 * cg))
            tasks.append(("D", 2 * cg + 1))
        projs, attns = [], []
        for k, (kind, idx) in enumerate(tasks):
            st_ = k % 2
            if kind == "R":
                projs.append(proj_units_R(idx, st_))
                attns.append(attn_units_R(idx, st_, spawn))
            else:
                sl_ = dcount % 2
                dcount += 1
                projs.append(proj_units_D(idx, st_, sl_))
                attns.append(attn_units_D(idx, st_, sl_, spawn))
        def run_side_only(side):
            si = 0
            next_start = 0
            mi = 0
            while si < len(side) or active:
                advance()
                if si < len(side) and mi >= next_start:
                    g = side[si][0]()
                    next_start = mi + side[si][1]
                    si += 1
                    spawn(g)
                    try:
                        next(g)
                    except StopIteration:
                        active.remove(g)
                mi += 1
        if os.environ.get("KSO", "1") == "1":
            run_side_only(projs[0])
        else:
            for (u, per) in projs[0]:
                for _ in u():
                    pass
        for k in range(len(tasks)):
            main = attns[k]
            side = projs[k + 1] if k + 1 < len(tasks) else []
            fast = tasks[k][0] == "R"
            si = 0
            next_start = 0
            for mi, u in enumerate(main):
                u()
                advance()
                if si < len(side) and mi >= next_start:
                    g = side[si][0]()
                    next_start = mi + (side[si][2] if (fast and len(side[si]) > 2) else side[si][1])
                    si += 1
                    spawn(g)
                    try:
                        next(g)
                    except StopIteration:
                        active.remove(g)
            while si < len(side):
                g = side[si][0]()
                si += 1
                for _ in g:
                    pass
            while active:
                advance()
        sch.barrier()

        if os.environ.get("KP2", "1") == "0":
            continue
        tb_rot = Rot([0, 1])
        for tt in range(NT):
            b = tb_rot.next()
            pt = bk16(b).rearrange("p (k n) -> p k n", k=8)

            def tr(e, tt=tt, pt=pt):
                ins = None
                for kc in range(8):
                    ins = e.transpose(out=pt[:, kc, :], in_=merged[:, tt, kc * 128:(kc + 1) * 128], identity=ident)
                return ins
            add("pe", tr, r=[("mg", tt), ("c_ident",)], w=[("ps", b)])
            eng = "act" if tt % 2 == 0 else "dve"
            if eng == "act":
                add("act", lambda e, pt=pt, tt=tt: e.activation(out=hT[:, :, tt * 128:(tt + 1) * 128], in_=pt, func=AF.Copy),
                    r=[("ps", b)], w=[("hT", tt)])
            else:
                add("dve", lambda e, pt=pt, tt=tt: e.tensor_copy(hT[:, :, tt * 128:(tt + 1) * 128], pt),
                    r=[("ps", b)], w=[("hT", tt)])
        sch.barrier()
        pj = Rot([2, 3])
        gu = Rot([4, 5, 6, 7])
        xs_rot = Rot([0, 1, 2, 3])
        ys_rot = Rot([0, 1])
        sg_rot = Rot([0, 1])
        wd_rot = Rot([0, 1])
        for qd in range(4):
            for c2 in range(2):
                wsl = wstream.take(("wo", c2))
                wstream.prefetch()
                for tl in range(4):
                    tt = 4 * qd + tl
                    xsl = xs_rot.next()
                    dma("sp", xs2[xsl], x_d[s, tt * 128:(tt + 1) * 128, 512 * c2:512 * c2 + 512], [], [("xs2", xsl)], "xs2_%d" % xsl)
                    b = pj.next()

                    def op(e, b=b, tt=tt, wsl=wsl):
                        ins = None
                        for kc in range(8):
                            ins = e.matmul(bk(b), lhsT=hT[:, kc, tt * 128:(tt + 1) * 128], rhs=wring[wsl][:, kc, :],
                                           start=(kc == 0), stop=(kc == 7))
                        return ins
                    add("pe", op, r=[("W", wsl), ("hT", tt)], w=[("ps", b)])
                    add("dve", lambda e, b=b, tl=tl, c2=c2, xsl=xsl: e.tensor_tensor(out=x1[:, tl, 512 * c2:512 * c2 + 512], in0=bk(b),
                                                                                   in1=xs2[xsl], op=ALU.add),
                        r=[("ps", b), ("xs2", xsl)], w=[("x1", tl, c2)])
            for tl in range(4):
                sl = tl % 2
                st = stat[:, 48 + 4 * sl:52 + 4 * sl]
                add("act", lambda e, tl=tl, st=st: e.activation(out=junk, in_=x1[:, tl, :], func=AF.Square, accum_out=st[:, 0:1]),
                    r=[("x1", tl)], w=[("junk",), ("st3", sl, 0)])
                add("act", lambda e, st=st: e.activation(out=st[:, 1:2], in_=st[:, 0:1], func=AF.Ln, scale=1.0 / D, bias=EPS),
                    r=[("st3", sl, 0)], w=[("st3", sl, 1)])
                add("act", lambda e, st=st: e.activation(out=st[:, 2:3], in_=st[:, 1:2], func=AF.Exp, scale=-0.5),
                    r=[("st3", sl, 1)], w=[("st3", sl, 2)])
                add("act", lambda e, tl=tl, sl=sl, st=st: e.activation(out=xh2[sl], in_=x1[:, tl, :], func=AF.Copy, scale=st[:, 2:3]),
                    r=[("x1", tl), ("st3", sl, 2)], w=[("xh2", sl)])
                b = tb_rot.next()
                pt = bk16(b).rearrange("p (k n) -> p k n", k=8)

                def tr(e, sl=sl, pt=pt):
                    ins = None
                    for kc in range(8):
                        ins = e.transpose(out=pt[:, kc, :], in_=xh2[sl][:, kc * 128:(kc + 1) * 128], identity=ident)
                    return ins
                add("pe", tr, r=[("xh2", sl), ("c_ident",)], w=[("ps", b)])
                add("dve", lambda e, pt=pt, tl=tl: e.tensor_tensor(out=h2T[:, :, tl * 128:(tl + 1) * 128], in0=pt,
                                                                    in1=g2t.unsqueeze(2).to_broadcast([128, 8, 128]), op=ALU.mult),
                    r=[("ps", b), ("c_g2t",)], w=[("h2T", tl)])
            for g in range(11):
                wsl = wstream.take(("gu", g))
                wstream.prefetch()
                for fl in range(2):
                    fc = 2 * g + fl
                    bg = gu.next()
                    bu = gu.next()

                    def gup(e, wsl=wsl, fl=fl, bg=bg, bu=bu):
                        ins = None
                        for kc in range(8):
                            e.matmul(bk(bg), lhsT=wring[wsl][:, kc, fl * 128:(fl + 1) * 128], rhs=h2T[:, kc, :], start=(kc == 0), stop=(kc == 7))
                        for kc in range(8):
                            ins = e.matmul(bk(bu), lhsT=wring[wsl][:, kc, 256 + fl * 128:256 + (fl + 1) * 128], rhs=h2T[:, kc, :],
                                           start=(kc == 0), stop=(kc == 7))
                        return ins
                    add("pe", gup, r=[("W", wsl), ("h2T",)], w=[("ps", bg), ("ps", bu)])
                    si = sg_rot.next()
                    add("act", lambda e, bg=bg, si=si: e.activation(out=sgb[si], in_=bk(bg), func=AF.Silu),
                        r=[("ps", bg)], w=[("sgb", si)])
                    add("dve", lambda e, bu=bu, si=si, fc=fc: e.tensor_tensor(out=actT[:, fc, :], in0=bk(bu), in1=sgb[si], op=ALU.mult),
                        r=[("ps", bu), ("sgb", si)], w=[("actT", fc)])
            for c2 in range(2):
                wdl = wd_rot.next()
                dma("sp", WD[wdl], wd_s[c2].rearrange("(f p) c -> p f c", p=128), [], [("WD", wdl)], "wd%d" % wdl)
                for tl in range(4):
                    tt = 4 * qd + tl
                    b = pj.next()

                    def dn_(e, b=b, tl=tl, wdl=wdl):
                        ins = None
                        for fc in range(NFC):
                            ins = e.matmul(bk(b), lhsT=actT[:, fc, tl * 128:(tl + 1) * 128], rhs=WD[wdl][:, fc, :],
                                           start=(fc == 0), stop=(fc == NFC - 1))
                        return ins
                    add("pe", dn_, r=[("WD", wdl), ("actT",)], w=[("ps", b)])
                    ysl = ys_rot.next()
                    add("dve", lambda e, b=b, tl=tl, c2=c2, ysl=ysl: e.tensor_tensor(out=yst[ysl], in0=bk(b), in1=x1[:, tl, 512 * c2:512 * c2 + 512],
                                                                                   op=ALU.add),
                        r=[("ps", b), ("x1", tl, c2)], w=[("yst", ysl)])
                    dma("pool", y_d[s, tt * 128:(tt + 1) * 128, 512 * c2:512 * c2 + 512], yst[ysl], [("yst", ysl)], [("ydram", ysl)], "yo%d" % ysl)
        sch.barrier()

    add("pool", None, r=[("ydram",)], w=[])
    sch.emit(nc, stack)
    stack.close()
    return nc


_PROG_CACHE = {}


def _run(xs_per_core, wts, consts):
    nseq = xs_per_core[0].shape[0]
    if nseq not in _PROG_CACHE:
        _PROG_CACHE[nseq] = build_program(nseq)
    nc = _PROG_CACHE[nseq]
    in_maps = []
    for c in range(NCORES):
        m = dict(wts)
        m.update(consts)
        m["x"] = xs_per_core[c]
        in_maps.append(m)
    res = run_bass_kernel_spmd(nc, in_maps, core_ids=list(range(NCORES)))
    return [r["y"] for r in res.results]


def kernel(x_prompt, x_sample, rel_bias_table, norm_mix_g, w_in, ret_decay_fwd, ret_decay_bwd,
           q_norm_g, k_norm_g, lam_q1, lam_k1, lam_q2, lam_k2, subln_g, w_out, norm_ffn_g,
           w_gate, w_up, w_down):
    f32 = np.float32
    A = lambda a: np.ascontiguousarray(np.asarray(a, dtype=f32))
    x_prompt = A(x_prompt)
    x_sample = A(x_sample)
    consts = _host_consts()
    wts = {
        "w_in": A(w_in)[0], "w_out": A(w_out)[0], "w_gate": A(w_gate)[0], "w_up": A(w_up)[0], "w_down": A(w_down)[0],
        "tab": A(rel_bias_table),
        "tabb": A(np.broadcast_to(A(rel_bias_table).reshape(1, 256), (128, 256))),
        "dec": A(np.broadcast_to(np.concatenate([A(ret_decay_fwd)[0], A(ret_decay_bwd)[0]])[None, :], (128, 8))),
        "lamv": A(np.broadcast_to(np.stack([A(lam_q1)[0], A(lam_q2)[0], A(lam_k1)[0], A(lam_k2)[0]])[None], (128, 4, 64))),
        "gqk": A(np.stack([np.tile(A(q_norm_g)[0], 2), np.tile(A(k_norm_g)[0], 2)], axis=1)),
        "subln": A(np.broadcast_to(np.tile(A(subln_g)[0], 2)[None, :], (128, 256))),
        "g1t": A(A(norm_mix_g)[0].reshape(8, 128).T),
        "g2t": A(A(norm_ffn_g)[0].reshape(8, 128).T),
    }
    nP = x_prompt.shape[0] // NCORES
    nS = x_sample.shape[0] // NCORES
    xs = [np.ascontiguousarray(np.concatenate([x_prompt[c * nP:(c + 1) * nP], x_sample[c * nS:(c + 1) * nS]], axis=0))
          for c in range(NCORES)]
    ys = _run(xs, wts, consts)
    y_prompt = np.concatenate([y[:nP] for y in ys], axis=0).astype(f32)
    y_sample = np.concatenate([y[nP:] for y in ys], axis=0).astype(f32)
    return (y_prompt, y_sample)
```

```python
import math
import os
import numpy as np
import ml_dtypes
import concourse.bass as bass
import concourse.mybir as mybir
from concourse.bass_utils import run_bass_kernel_spmd

F32 = mybir.dt.float32
BF16 = mybir.dt.bfloat16
AF = mybir.ActivationFunctionType
ALU = mybir.AluOpType
AX = mybir.AxisListType

NCORES = 8
S = 2048
D = 1024
NT = 16
DFF = 2816
NFC = 22
EPS = 1e-6
LAMBDA_INIT = 0.8 - 0.6 * math.exp(-0.3 * 0)

C_RQ, C_RK, C_RV, C_RG, C_DQ, C_DK, C_DV, C_MGR, C_MGD = 0, 512, 1024, 2048, 3072, 4096, 5120, 6144, 7168


class _Op:
    __slots__ = ("eng", "fn", "deps", "dma_key", "dma_cnt", "ms", "has_dep")

    def __init__(self, eng, fn, deps, dma_key):
        self.eng = eng
        self.fn = fn
        self.deps = deps
        self.dma_key = dma_key
        self.dma_cnt = 0
        self.ms = 0
        self.has_dep = False


class Sched:
    ENGS = ("pe", "act", "dve", "pool", "sp")

    def __init__(self):
        self.ops = []
        self.lw = {}
        self.rd = {}
        self.roots = {}
        self.dma_total = {}
        self.last_on_eng = {e: None for e in self.ENGS}
        self.barrier_deps = {e: set() for e in self.ENGS}
        self.dma_last = {}
        self.dma_hist = {}

    def _related(self, key):
        ks = self.roots.get(key[0])
        if not ks:
            return ()
        n = len(key)
        out = []
        for k in ks:
            m = min(n, len(k))
            if k[:m] == key[:m]:
                out.append(k)
        return out

    def add(self, eng, fn, r=(), w=(), dma_key=None):
        idx = len(self.ops)
        deps = set(self.barrier_deps[eng])
        self.barrier_deps[eng] = set()
        for key in r:
            for k in self._related(key):
                lw = self.lw.get(k)
                if lw is not None:
                    deps.add(lw)
        for key in w:
            for k in self._related(key):
                lw = self.lw.get(k)
                if lw is not None:
                    deps.add(lw)
                rr = self.rd.get(k)
                if rr:
                    deps.update(rr.values())
        is_dma = dma_key is not None
        for key in r:
            self.roots.setdefault(key[0], set()).add(key)
            d = self.rd.setdefault(key, {})
            if is_dma:
                d[("dma", idx)] = idx
            else:
                d[eng] = idx
        for key in w:
            self.roots.setdefault(key[0], set()).add(key)
            for k in self._related(key):
                if len(k) > len(key):
                    self.lw[k] = idx
                    self.rd[k] = {}
            self.lw[key] = idx
            self.rd[key] = {}
        deps.discard(idx)
        if eng == "pe" and os.environ.get("KPE", "0") == "0":
            deps = set(j for j in deps if not (self.ops[j].eng == "pe" and self.ops[j].dma_key is None))
        op = _Op(eng, fn, deps, dma_key)
        if is_dma:
            self.dma_total[dma_key] = self.dma_total.get(dma_key, 0) + 16
            op.dma_cnt = self.dma_total[dma_key]
            self.dma_last[dma_key] = idx
            self.dma_hist.setdefault(dma_key, []).append((idx, op.dma_cnt))
        self.ops.append(op)
        self.last_on_eng[eng] = idx
        return idx

    def barrier(self):
        lasts = set(i for i in self.last_on_eng.values() if i is not None)
        lasts.update(self.dma_last.values())
        for e in self.ENGS:
            self.barrier_deps[e].update(lasts)

    def emit(self, nc, stack):
        ops = self.ops
        for op in ops:
            for j in op.deps:
                ops[j].has_dep = True
        cnt = {e: 0 for e in self.ENGS}
        for op in ops:
            if op.dma_key is None and op.has_dep:
                cnt[op.eng] += 1
                op.ms = cnt[op.eng]
        esem = {e: stack.enter_context(nc.semaphore("sem_" + e)) for e in self.ENGS}
        dsem = {k: stack.enter_context(nc.semaphore("dsem_%d" % i)) for i, k in enumerate(self.dma_total)}
        per_eng = {e: [] for e in self.ENGS}
        for op in ops:
            per_eng[op.eng].append(op)

        import bisect
        opidx = {id(op): i for i, op in enumerate(ops)}

        if os.environ.get("KCHECK", "0") == "1":
            done = [False] * len(ops)
            ptr = {e: 0 for e in self.ENGS}
            progress = True
            while progress:
                progress = False
                for e in self.ENGS:
                    while ptr[e] < len(per_eng[e]):
                        op = per_eng[e][ptr[e]]
                        if all(done[j] for j in op.deps):
                            done[opidx[id(op)]] = True
                            ptr[e] += 1
                            progress = True
                        else:
                            break
            for e in self.ENGS:
                if ptr[e] < len(per_eng[e]):
                    op = per_eng[e][ptr[e]]
                    print("DEADLOCK", e, ptr[e], len(per_eng[e]), "op", opidx[id(op)], "waits", [(j, ops[j].eng) for j in op.deps if not done[j]])
            print("KCHECK ok", {e: len(per_eng[e]) for e in self.ENGS})

        def run(eng_name, e):
            waited = {}
            for op in per_eng[eng_name]:
                need = {}
                me = opidx[id(op)]
                for j in op.deps:
                    d = ops[j]
                    if d.dma_key is not None:
                        hist = self.dma_hist[d.dma_key]
                        pos = bisect.bisect_left(hist, (me, 0)) - 1
                        sem, val = dsem[d.dma_key], max(d.dma_cnt, hist[pos][1] if pos >= 0 else 0)
                    else:
                        sem, val = esem[d.eng], d.ms
                    kk = id(sem)
                    if val > need.get(kk, (None, 0))[1]:
                        need[kk] = (sem, val)
                for kk, (sem, val) in need.items():
                    if waited.get(kk, 0) < val:
                        e.wait_ge(sem, val)
                        waited[kk] = val
                if op.fn is None:
                    continue
                ins = op.fn(e)
                if op.dma_key is not None:
                    ins.then_inc(dsem[op.dma_key], 16)
                elif op.has_dep:
                    ins.then_inc(esem[eng_name], 1)

        block = stack.enter_context(nc.Block())

        @block.tensor
        def _(e):
            run("pe", e)

        @block.scalar
        def _(e):
            run("act", e)

        @block.vector
        def _(e):
            run("dve", e)

        @block.gpsimd
        def _(e):
            run("pool", e)

        @block.sync
        def _(e):
            run("sp", e)


class Rot:
    def __init__(self, items):
        self.items = list(items)
        self.i = 0

    def next(self):
        v = self.items[self.i % len(self.items)]
        self.i += 1
        return v


def _t5_bucket_np(rel):
    rel = np.asarray(rel, dtype=np.int64)
    nb = 16
    max_exact = 8
    ret = np.where(rel > 0, nb, 0)
    n = np.abs(rel)
    nf = np.maximum(n, 1).astype(np.float32)
    large = max_exact + (np.log(nf / np.float32(max_exact)) / np.float32(math.log(128 / max_exact))
                         * np.float32(nb - max_exact)).astype(np.int32)
    for nn, b in ((16, 10), (32, 12), (64, 14)):
        large = np.where(n == nn, b, large)
    large = np.minimum(large, nb - 1)
    return ret + np.where(n < max_exact, n, large)


def _host_consts():
    bf = ml_dtypes.bfloat16
    c = {}
    eye = np.eye(128, dtype=np.float32)
    c["ident"] = eye.astype(bf)
    c["antii"] = eye[::-1].copy().astype(bf)
    p = np.arange(128)
    blk = ((p[:, None] // 64) == (p[None, :] // 64)).astype(np.float32) / 64.0
    c["blk"] = blk.astype(bf)
    half = 64
    inv = (10000.0 ** (-np.arange(half, dtype=np.float32) / half)).astype(np.float32)
    ang = np.arange(S, dtype=np.float32)[None, :] * inv[:, None]
    cos = np.cos(ang).astype(np.float32)
    sin = np.sin(ang).astype(np.float32)
    cs = np.zeros((128, 2, S), np.float32)
    cs[:64, 0] = cos
    cs[64:, 0] = cos
    cs[:64, 1] = -sin
    cs[64:, 1] = sin
    c["cs"] = cs.astype(bf)
    t = np.arange(767)
    b = _t5_bucket_np(383 - t)
    oh = np.zeros((32, 767), np.float32)
    oh[b, t] = 1.0
    c["oh1"] = oh
    cf = np.zeros((128, 16 + 256 + 256 + 640 + 2 + 256), np.float32)
    cf[:, 0:16] = (128.0 * np.arange(16))[None, :]
    jl = np.arange(128)[:, None].astype(np.float32)
    il = np.arange(256)[None, :].astype(np.float32)
    cf[:, 16:272] = il - jl + 128.0
    cf[:, 272:528] = jl - il + 256.0
    cc = np.arange(640)[None, :].astype(np.float32)
    cf[:, 528:1168] = jl - cc + 256.0
    cf[:, 1168] = 127.0 - np.arange(128)
    cf[:, 1169] = np.arange(128)
    cf[:, 1170:1298] = (np.arange(128) + 1.0)[None, :]
    cf[:, 1298:1426] = (128.0 - np.arange(128))[None, :]
    c["cf32"] = cf
    m01 = np.zeros((128, 2), np.float32)
    m01[:64, 0] = 1.0
    m01[64:, 1] = 1.0
    c["m01"] = m01
    return c


def build_program(nseq):
    from contextlib import ExitStack
    nc = bass.Bass("TRN2", target_bir_lowering=False)
    stack = ExitStack()
    sch = Sched()

    def din(name, shape, dt=F32):
        return nc.dram_tensor(name, list(shape), dt, kind="ExternalInput").ap()

    x_d = din("x", [nseq, S, D])
    w_in_d = din("w_in", [D, 8192])
    w_out_d = din("w_out", [D, D])
    w_gate_d = din("w_gate", [D, DFF])
    w_up_d = din("w_up", [D, DFF])
    w_down_d = din("w_down", [DFF, D])
    ident_d = din("ident", [128, 128], BF16)
    antii_d = din("antii", [128, 128], BF16)
    blk_d = din("blk", [128, 128], BF16)
    cs_d = din("cs", [128, 2, S], BF16)
    oh1_d = din("oh1", [32, 767])
    cf32_d = din("cf32", [128, 1426])
    m01_d = din("m01", [128, 2])
    tab_d = din("tab", [32, 8])
    tabb_d = din("tabb", [128, 256])
    dec_d = din("dec", [128, 8])
    lamv_d = din("lamv", [128, 4, 64])
    gqk_d = din("gqk", [128, 2])
    subln_d = din("subln", [128, 256])
    g1t_d = din("g1t", [128, 8])
    g2t_d = din("g2t", [128, 8])
    y_d = nc.dram_tensor("y", [nseq, S, D], F32, kind="ExternalOutput").ap()

    win_s = nc.dram_tensor("win_s", [20, D, 512], BF16, kind="Internal").ap()
    wo_s = nc.dram_tensor("wo_s", [2, D, 512], BF16, kind="Internal").ap()
    wgu_s = nc.dram_tensor("wgu_s", [11, D, 512], BF16, kind="Internal").ap()
    wd_s = nc.dram_tensor("wd_s", [2, DFF, 512], BF16, kind="Internal").ap()
    gvb_t = nc.dram_tensor("gv_s", [8, 767], F32, kind="Internal")
    gv_s = gvb_t.ap()

    TOT = 204 * 1024
    raw = nc.alloc_sbuf_tensor("raw", [128, TOT // 2], BF16)
    cur = [0]

    def alloc(nbytes, dt, pattern=None, base=None, **kw):
        if base is None:
            off = cur[0]
            cur[0] += (nbytes + 63) // 64 * 64
            assert cur[0] <= TOT, ("sbuf overflow", cur[0])
        else:
            off = base
        v = raw[:, off // 2:(off + nbytes) // 2]
        if dt is F32:
            v = v.bitcast(F32)
        if pattern:
            v = v.rearrange(pattern, **kw)
        return v

    ident = alloc(256, BF16)
    antii = alloc(256, BF16)
    blk = alloc(256, BF16)
    cs = alloc(2 * S * 2, BF16, "p (a n) -> p a n", a=2)
    dn = alloc(4 * 640 * 2, BF16, "p (h c) -> p h c", h=4)
    Xi = alloc(8 * 128 * 2, BF16, "p (h c) -> p h c", h=8)
    zt = alloc(8 * 4, F32)
    cdc = alloc(8 * 4, F32)
    pws = alloc(8 * 16 * 4, F32, "p (h m) -> p h m", h=8)
    tabb = alloc(256 * 4, F32)
    lg = alloc(16 * 4, F32)
    gcols = alloc(8 * 4, F32)
    lamc = alloc(8 * 4, F32)
    sg8 = alloc(256 * 4, F32)
    g1t = alloc(8 * 4, F32)
    g2t = alloc(8 * 4, F32)
    stat = alloc(64 * 4, F32)
    hT = alloc(8 * S * 2, BF16, "p (k n) -> p k n", k=8)
    merged_base = cur[0]
    merged = alloc(NT * D * 2, BF16, "p (t c) -> p t c", t=NT)
    wring = [alloc(8 * 512 * 2, BF16, "p (k c) -> p k c", k=8) for _ in range(2)]
    U0 = cur[0]
    print("fixed sbuf bytes", U0, "U bytes", TOT - U0)

    cur[0] = U0
    E = [alloc(16 * 512 * 2, BF16, "p (j c) -> p j c", j=16) for _ in range(2)]
    qsets = [tuple(alloc(S * 2, BF16) for _ in range(3)) for _ in range(2)]
    V = alloc(NT * 256 * 2, BF16, "p (t c) -> p t c", t=NT)
    VA = [alloc(NT * 130 * 2, BF16, "p (t c) -> p t c", t=NT) for _ in range(2)]
    gate_ret = alloc(NT * 256 * 2, BF16, "p (t c) -> p t c", t=NT)
    gate_diff = [alloc(NT * 128 * 2, BF16, "p (t c) -> p t c", t=NT) for _ in range(2)]
    hkb1 = alloc(640 * 2, BF16)
    ehk = [alloc(640 * 2, BF16) for _ in range(2)]
    tA = [alloc(512 * 4, F32) for _ in range(2)]
    tB = [alloc(512 * 4, F32) for _ in range(2)]
    sqb = [alloc(512 * 2, BF16)] * 2
    tP = [alloc(128 * 4, F32) for _ in range(2)]
    osb = [alloc(128 * 4, F32) for _ in range(2)]
    tmpb = [alloc(128 * 2, BF16) for _ in range(2)]
    jk = [alloc(256 * 2, BF16) for _ in range(2)]
    P1_END = cur[0]
    xs0 = [alloc(D * 4, F32, base=merged_base + i * 4096) for i in range(2)]
    xh0 = [alloc(D * 2, BF16, base=merged_base + 8192 + i * 2048) for i in range(2)]
    cur[0] = U0
    WD = [alloc(NFC * 512 * 2, BF16, "p (f c) -> p f c", f=NFC) for _ in range(2)]
    actT = alloc(NFC * 512 * 2, BF16, "p (f c) -> p f c", f=NFC)
    sgb = [alloc(512 * 4, F32) for _ in range(2)]
    xs2 = [alloc(512 * 4, F32) for _ in range(4)]
    yst = [alloc(512 * 4, F32) for _ in range(2)]
    xh2 = [alloc(D * 2, BF16) for _ in range(2)]
    junk = alloc(D * 2, BF16)
    P2_END = cur[0]
    x1 = alloc(4 * D * 4, F32, "p (t c) -> p t c", t=4, base=merged_base)
    h2T = alloc(8 * 512 * 2, BF16, "p (k n) -> p k n", k=8, base=merged_base + 16384)
    print("phase1 end", P1_END, "phase2 end", P2_END, "limit", TOT)
    assert P1_END <= TOT and P2_END <= TOT

    banks = [nc.alloc_psum_tensor("pb%d" % i, [128, 512], F32) for i in range(8)]

    def bk(i):
        return banks[i][:, :]

    def bk16(i):
        return banks[i][:, :].bitcast(BF16)

    add = sch.add

    def dma(eng, out, in_, r, w, key):
        add(eng, lambda e, o=out, i=in_: e.dma_start(out=o, in_=i), r=r, w=w, dma_key=key)

    dma("sp", ident, ident_d, [], [("c_ident",)], "c0")
    dma("sp", antii, antii_d, [], [("c_antii",)], "c0")
    dma("sp", blk, blk_d, [], [("c_blk",)], "c0")
    dma("sp", cs, cs_d, [], [("c_cs",)], "c0")
    dma("sp", tabb, tabb_d, [], [("c_tabb",)], "c0")
    dma("sp", g1t, g1t_d, [], [("c_g1t",)], "c0")
    dma("sp", g2t, g2t_d, [], [("c_g2t",)], "c0")
    dma("sp", sg8, subln_d, [], [("c_sg8",)], "c0")
    cur[0] = U0
    cf = alloc(1426 * 4, F32)
    t1s = alloc(640 * 4, F32)
    t2s = alloc(640 * 4, F32)
    lamv = alloc(256 * 4, F32, "p (a c) -> p a c", a=4)
    prod = alloc(128 * 4, F32, "p (a c) -> p a c", a=2)
    decs = alloc(8 * 4, F32)
    gqk = alloc(2 * 4, F32)
    m01 = alloc(2 * 4, F32)
    tab32 = alloc(8 * 4, F32)
    oh1 = alloc(768 * 4, F32)
    gvsb = alloc(768 * 4, F32)
    stg = [alloc(8 * 512 * 2, BF16, "p (k c) -> p k c", k=8) for _ in range(4)]
    stgf = [alloc(8 * 512 * 4, F32, "p (k c) -> p k c", k=8) for _ in range(3)]
    assert cur[0] <= TOT
    dma("sp", cf, cf32_d, [], [("s_cf",)], "c1")
    dma("sp", lamv, lamv_d, [], [("s_lamv",)], "c1")
    dma("sp", decs, dec_d, [], [("s_dec",)], "c1")
    dma("sp", gqk, gqk_d, [], [("s_gqk",)], "c1")
    dma("sp", m01, m01_d, [], [("s_m01",)], "c1")
    dma("sp", tab32[0:32, :], tab_d, [], [("s_tab",)], "c1")
    dma("sp", oh1[0:32, 0:767], oh1_d, [], [("s_oh1",)], "c1")

    add("act", lambda e: e.activation(out=stat[:, 0:8], in_=decs, func=AF.Exp), r=[("s_dec",)], w=[("st", 0)])
    add("dve", lambda e: e.tensor_scalar(out=stat[:, 8:16], in0=stat[:, 0:8], scalar1=-1.0, scalar2=1.0,
                                         op0=ALU.mult, op1=ALU.add), r=[("st", 0)], w=[("st", 1)])
    add("act", lambda e: e.activation(out=lg[:, 0:8], in_=stat[:, 8:16], func=AF.Ln), r=[("st", 1)], w=[("c_lg", 0)])
    add("dve", lambda e: e.tensor_scalar(out=lg[:, 8:16], in0=lg[:, 0:8], scalar1=-1.0, scalar2=None, op0=ALU.mult),
        r=[("c_lg", 0)], w=[("c_lg", 1)])
    lnsc = math.log(128.0 ** -0.5)
    add("act", lambda e: e.activation(out=cdc, in_=lg[:, 0:8], func=AF.Exp, scale=128.0), r=[("c_lg", 0)], w=[("c_cdc",)])
    for idx in range(8):
        add("act", lambda e, idx=idx: e.activation(out=pws[:, idx, :], in_=cf[:, 0:16], func=AF.Exp,
                                                    scale=lg[:, idx:idx + 1], bias=lnsc),
            r=[("s_cf",), ("c_lg", 0)], w=[("c_pws", idx)])
    for r_ in range(4):
        add("act", lambda e, r_=r_: e.activation(out=Xi[:, r_, :], in_=cf[:, 1170:1298], func=AF.Exp, scale=lg[:, r_:r_ + 1]),
            r=[("s_cf",), ("c_lg", 0)], w=[("c_xi", r_)])
        add("act", lambda e, r_=r_: e.activation(out=Xi[:, 4 + r_, :], in_=cf[:, 1298:1426], func=AF.Exp, scale=lg[:, 4 + r_:5 + r_]),
            r=[("s_cf",), ("c_lg", 0)], w=[("c_xi", 4 + r_)])
        add("act", lambda e, r_=r_: e.activation(out=zt[:, r_:r_ + 1], in_=cf[:, 1168:1169], func=AF.Exp, scale=lg[:, r_:r_ + 1], bias=lnsc),
            r=[("s_cf",), ("c_lg", 0)], w=[("c_zt", r_)])
        add("act", lambda e, r_=r_: e.activation(out=zt[:, 4 + r_:5 + r_], in_=cf[:, 1169:1170], func=AF.Exp, scale=lg[:, 4 + r_:5 + r_], bias=lnsc),
            r=[("s_cf",), ("c_lg", 0)], w=[("c_zt", 4 + r_)])
        add("dve", lambda e, r_=r_: e.tensor_scalar(out=t1s, in0=cf[:, 528:1168], scalar1=lg[:, 8 + r_:9 + r_],
                                                     scalar2=None, op0=ALU.mult),
            r=[("s_cf",), ("c_lg", 1)], w=[("s_t1",)])
        add("dve", lambda e, r_=r_: e.scalar_tensor_tensor(out=t2s, in0=cf[:, 528:1168], scalar=lg[:, 4 + r_:5 + r_],
                                                            in1=t1s, op0=ALU.mult, op1=ALU.min),
            r=[("s_cf",), ("c_lg", 0), ("s_t1",)], w=[("s_t2",)])
        add("act", lambda e, r_=r_: e.activation(out=dn[:, r_, :], in_=t2s, func=AF.Exp),
            r=[("s_t2",)], w=[("c_dn", r_)])
    add("dve", lambda e: e.tensor_tensor(out=prod, in0=lamv[:, 0:2, :], in1=lamv[:, 2:4, :], op=ALU.mult),
        r=[("s_lamv",)], w=[("s_prod",)])
    add("dve", lambda e: e.reduce_sum(out=lamc[:, 0:2], in_=prod, axis=AX.X), r=[("s_prod",)], w=[("c_lam", 0)])
    add("act", lambda e: e.activation(out=lamc[:, 2:4], in_=lamc[:, 0:2], func=AF.Exp), r=[("c_lam", 0)], w=[("c_lam", 1)])
    add("dve", lambda e: e.tensor_tensor(out=lamc[:, 4:5], in0=lamc[:, 3:4], in1=lamc[:, 2:3], op=ALU.subtract),
        r=[("c_lam", 1)], w=[("c_lam", 2)])
    add("dve", lambda e: e.tensor_scalar(out=lamc[:, 5:6], in0=lamc[:, 4:5], scalar1=-LAMBDA_INIT, scalar2=None, op0=ALU.add),
        r=[("c_lam", 2)], w=[("c_nlam",)])
    NLAM = lamc[:, 5:6]
    add("dve", lambda e: e.tensor_scalar(out=gcols[:, 0:1], in0=gqk[:, 0:1], scalar1=0.125, scalar2=None, op0=ALU.mult),
        r=[("s_gqk",)], w=[("c_gc", 0)])
    add("dve", lambda e: e.tensor_tensor(out=gcols[:, 1:3], in0=m01, in1=gqk[:, 1:2].to_broadcast([128, 2]), op=ALU.mult),
        r=[("s_gqk",), ("s_m01",)], w=[("c_gc", 1)])
    add("dve", lambda e: e.tensor_scalar(out=sg8, in0=sg8, scalar1=1.0 - LAMBDA_INIT, scalar2=None, op0=ALU.mult),
        r=[("c_sg8",)], w=[("c_sg8",)])
    add("pe", lambda e: (e.matmul(banks[0][0:8, 0:512], lhsT=tab32[0:32, 0:8], rhs=oh1[0:32, 0:512], start=True, stop=True),
                         e.matmul(banks[1][0:8, 0:255], lhsT=tab32[0:32, 0:8], rhs=oh1[0:32, 512:767], start=True, stop=True))[1],
        r=[("s_tab",), ("s_oh1",)], w=[("ps", 0), ("ps", 1)])
    add("dve", lambda e: e.tensor_copy(gvsb[0:8, 0:512], banks[0][0:8, 0:512]), r=[("ps", 0)], w=[("s_gv", 0)])
    add("dve", lambda e: e.tensor_copy(gvsb[0:8, 512:767], banks[1][0:8, 0:255]), r=[("ps", 1)], w=[("s_gv", 1)])
    dma("sp", gv_s, gvsb[0:8, 0:767], [("s_gv",)], [("d_gv",)], "c2")

    stg_rot = Rot(list(range(4)))
    stgf_rot = Rot(list(range(3)))
    cv_rot = Rot(["dve", "pool", "dve"])
    wscr_n = [0]

    def cast_group(dst, pieces, nk=8):
        sl = stg_rot.next()
        sf = stgf_rot.next()
        for (src, off, n) in pieces:
            dma("sp", stgf[sf][:, 0:nk, off:off + n], src, [], [("stgf", sf)], "stgf%d" % sf)
        ncol = max(off + n for (_, off, n) in pieces)
        add(cv_rot.next(), lambda e: e.tensor_copy(stg[sl][:, 0:nk, 0:ncol], stgf[sf][:, 0:nk, 0:ncol]),
            r=[("stgf", sf)], w=[("stg", sl)])
        wscr_n[0] += 1
        dma("act", dst, stg[sl][:, 0:nk, :], [("stg", sl)], [("wscr", wscr_n[0])], "stgo%d" % sl)

    def wcols(wd, c0, n):
        return wd[:, c0:c0 + n].rearrange("(k p) c -> p k c", p=128)

    for cg in range(4):
        r_ = cg
        q0 = C_RQ + 128 * r_
        k0 = C_RK + 128 * r_
        g0 = [(wcols(w_in_d, q0, 128), 0, 128),
              (wcols(w_in_d, q0 + 64, 64), 128, 64), (wcols(w_in_d, q0, 64), 192, 64),
              (wcols(w_in_d, k0, 128), 256, 128),
              (wcols(w_in_d, k0 + 64, 64), 384, 64), (wcols(w_in_d, k0, 64), 448, 64)]
        g1 = [(wcols(w_in_d, C_RV + 256 * r_, 256), 0, 256), (wcols(w_in_d, C_RG + 256 * r_, 256), 256, 256)]
        g2 = [(wcols(w_in_d, C_MGR + 256 * r_, 256), 0, 256), (wcols(w_in_d, C_MGR + 256 * r_, 256), 256, 256)]
        grp = [g0, g1, g2]
        for hh in range(2):
            h = 2 * cg + hh
            grp.append([(wcols(w_in_d, C_DQ + 128 * h, 128), 0, 128), (wcols(w_in_d, C_DK + 128 * h, 128), 128, 128),
                        (wcols(w_in_d, C_DV + 128 * h, 128), 256, 128), (wcols(w_in_d, C_MGD + 128 * h, 128), 384, 128)])
        for gi, pieces in enumerate(grp):
            cast_group(win_s[cg * 5 + gi].rearrange("(k p) c -> p k c", p=128), pieces)
    for c2 in range(2):
        cast_group(wo_s[c2].rearrange("(k p) c -> p k c", p=128), [(wcols(w_out_d, 512 * c2, 512), 0, 512)])
    for g in range(11):
        cast_group(wgu_s[g].rearrange("(k p) c -> p k c", p=128),
                   [(wcols(w_gate_d, 256 * g, 256), 0, 256), (wcols(w_up_d, 256 * g, 256), 256, 256)])
    for c2 in range(2):
        for (f0, nf) in ((0, 8), (8, 8), (16, 6)):
            src = w_down_d[f0 * 128:(f0 + nf) * 128, 512 * c2:512 * c2 + 512].rearrange("(k p) c -> p k c", p=128)
            dst = wd_s[c2][f0 * 128:(f0 + nf) * 128, :].rearrange("(k p) c -> p k c", p=128)
            cast_group(dst, [(src, 0, 512)], nk=nf)
    sch.barrier()

    def w_src(tag):
        kind, i = tag
        t = {"in": win_s, "wo": wo_s, "gu": wgu_s}[kind]
        return t[i].rearrange("(k p) c -> p k c", p=128)

    seq_wlist = []
    for cg_ in range(4):
        seq_wlist += [("in", cg_ * 5 + g_) for g_ in range(5)]
    for qd_ in range(4):
        seq_wlist += [("wo", 0), ("wo", 1)] + [("gu", g_) for g_ in range(11)]

    class WStream:
        def __init__(self, tags):
            self.tags = tags
            self.pos = 0
            self.issued = 0
            self.slot_of = {}

        def _issue(self):
            if self.issued < len(self.tags):
                sl = self.issued % 2
                dma("sp", wring[sl], w_src(self.tags[self.issued]), [], [("W", sl)], "w%d" % sl)
                self.slot_of[self.issued] = sl
                self.issued += 1

        def take(self, tag):
            while self.tags[self.pos] != tag:
                self.pos += 1
            if self.issued < self.pos:
                self.issued = self.pos
            while self.issued <= self.pos and self.issued < len(self.tags):
                self._issue()
            sl = self.slot_of[self.pos]
            self.pos += 1
            return sl

        def prefetch(self):
            while self.issued <= self.pos and self.issued < len(self.tags):
                self._issue()

    wstream = WStream(seq_wlist * nseq)

    for s in range(nseq):
        tb_rot = Rot([0, 1])
        for tt in range(NT):
            sl = tt % 2
            dma("sp", xs0[sl], x_d[s, tt * 128:(tt + 1) * 128, :], [], [("xs0", sl)], "xs0_%d" % sl)
            st = stat[:, 16 + 4 * sl:20 + 4 * sl]
            add("act", lambda e, sl=sl, st=st: e.activation(out=xh0[sl], in_=xs0[sl], func=AF.Square, accum_out=st[:, 0:1]),
                r=[("xs0", sl)], w=[("xh0", sl), ("st0", sl, 0)])
            add("act", lambda e, st=st: e.activation(out=st[:, 1:2], in_=st[:, 0:1], func=AF.Ln, scale=1.0 / D, bias=EPS),
                r=[("st0", sl, 0)], w=[("st0", sl, 1)])
            add("act", lambda e, st=st: e.activation(out=st[:, 2:3], in_=st[:, 1:2], func=AF.Exp, scale=-0.5),
                r=[("st0", sl, 1)], w=[("st0", sl, 2)])
            add("act", lambda e, sl=sl, st=st: e.activation(out=xh0[sl], in_=xs0[sl], func=AF.Copy, scale=st[:, 2:3]),
                r=[("xs0", sl), ("st0", sl, 2)], w=[("xh0", sl)])
            b = tb_rot.next()
            pt = bk16(b).rearrange("p (k n) -> p k n", k=8)

            def tr(e, sl=sl, pt=pt):
                ins = None
                for kc in range(8):
                    ins = e.transpose(out=pt[:, kc, :], in_=xh0[sl][:, kc * 128:(kc + 1) * 128], identity=ident)
                return ins
            add("pe", tr, r=[("xh0", sl), ("c_ident",)], w=[("ps", b)])
            add("dve", lambda e, pt=pt, tt=tt: e.tensor_tensor(out=hT[:, :, tt * 128:(tt + 1) * 128], in0=pt,
                                                                in1=g1t.unsqueeze(2).to_broadcast([128, 8, 128]), op=ALU.mult),
                r=[("ps", b), ("c_g1t",)], w=[("hT", tt)])
        sch.barrier()

        pj = Rot([0, 1, 4])
        sb = Rot([2, 3])
        ab = Rot([5, 6, 7])
        eslot = Rot([0, 1])
        tmp_rot = Rot([0, 1])
        pp_rot = Rot([0, 1])
        jk_rot = Rot([0, 1])

        def hT_keys(tb):
            return [("hT", 4 * tb + i) for i in range(4)]

        def proj_fm(wsl, c0, tb, b, m=128):
            def f(e):
                ins = None
                for kc in range(8):
                    ins = e.matmul(bk(b)[0:m, :], lhsT=wring[wsl][:, kc, c0:c0 + m], rhs=hT[:, kc, tb * 512:(tb + 1) * 512],
                                   start=(kc == 0), stop=(kc == 7))
                return ins
            add("pe", f, r=[("W", wsl)] + hT_keys(tb), w=[("ps", b)])

        def proj_tm(wsl, c0, n, tt, b):
            def f(e):
                ins = None
                for kc in range(8):
                    ins = e.matmul(bk(b)[:, 0:n], lhsT=hT[:, kc, tt * 128:(tt + 1) * 128], rhs=wring[wsl][:, kc, c0:c0 + n],
                                   start=(kc == 0), stop=(kc == 7))
                return ins
            add("pe", f, r=[("W", wsl), ("hT", tt)], w=[("ps", b)])

        def g_proj_fm(wsl, c0, tb, b):
            for half in range(2):
                def f(e, half=half):
                    ins = None
                    for kc in range(4 * half, 4 * half + 4):
                        ins = e.matmul(bk(b), lhsT=wring[wsl][:, kc, c0:c0 + 128], rhs=hT[:, kc, tb * 512:(tb + 1) * 512],
                                       start=(kc == 0), stop=(kc == 7))
                    return ins
                add("pe", f, r=[("W", wsl)] + hT_keys(tb), w=[("ps", b)])
                yield

        def g_proj_tm(wsl, c0, n, tt, b):
            for half in range(2):
                def f(e, half=half):
                    ins = None
                    for kc in range(4 * half, 4 * half + 4):
                        ins = e.matmul(bk(b)[:, 0:n], lhsT=hT[:, kc, tt * 128:(tt + 1) * 128], rhs=wring[wsl][:, kc, c0:c0 + n],
                                       start=(kc == 0), stop=(kc == 7))
                    return ins
                add("pe", f, r=[("W", wsl), ("hT", tt)], w=[("ps", b)])
                yield

        def proj_units_R(cg, st_):
            QT, K0T, K1T = qsets[st_]
            units = []
            wh = {}

            def getw(name, gi):
                if name not in wh:
                    wh[name] = wstream.take(("in", cg * 5 + gi))
                    wstream.prefetch()
                return wh[name]

            def u_rot(tb, c0, dst, dkey):
                def u():
                    wsl = getw("g0", 0)
                    tok = slice(tb * 512, (tb + 1) * 512)
                    b1 = pj.next()
                    yield from g_proj_fm(wsl, c0, tb, b1)
                    yield
                    ti = tmp_rot.next()
                    add("dve", lambda e: e.tensor_tensor(out=tA[ti], in0=bk(b1), in1=cs[:, 0, tok], op=ALU.mult),
                        r=[("ps", b1), ("c_cs",)], w=[("tA", ti)])
                    b2 = pj.next()
                    yield from g_proj_fm(wsl, c0 + 128, tb, b2)
                    yield
                    add("dve", lambda e: e.tensor_tensor(out=tB[ti], in0=bk(b2), in1=cs[:, 1, tok], op=ALU.mult),
                        r=[("ps", b2), ("c_cs",)], w=[("tB", ti)])
                    yield
                    add("pool", lambda e: e.tensor_tensor(out=dst[:, tok], in0=tA[ti], in1=tB[ti], op=ALU.add),
                        r=[("tA", ti), ("tB", ti)], w=[(dkey, st_, tb)])
                return u
            for tb in range(4):
                units.append((u_rot(tb, 0, QT, "QT"), 6))
                units.append((u_rot(tb, 256, K0T, "K0T"), 6))

            def u_v(tt):
                def u():
                    wsl = getw("g1", 1)
                    b = pj.next()
                    yield from g_proj_tm(wsl, 0, 512, tt, b)
                    yield
                    ti = tmp_rot.next()
                    add("act", lambda e: e.activation(out=tA[ti][:, 0:256], in_=bk(b)[:, 256:512], func=AF.Tanh, scale=0.5),
                        r=[("ps", b)], w=[("tA", ti)])
                    yield
                    add("dve", lambda e: e.tensor_copy(V[:, tt, :], bk(b)[:, 0:256]), r=[("ps", b), ("tA", ti)], w=[("V", tt)])
                    add("dve", lambda e: e.scalar_tensor_tensor(out=gate_ret[:, tt, :], in0=tA[ti][:, 0:256], scalar=1.0,
                                                                 in1=bk(b)[:, 256:512], op0=ALU.add, op1=ALU.mult),
                        r=[("ps", b), ("tA", ti)], w=[("gr", tt)])
                return u
            for tt in range(NT):
                units.append((u_v(tt), 3))

            def u_g(tt):
                def u():
                    wsl = getw("g2", 2)
                    b = pj.next()
                    yield from g_proj_tm(wsl, 0, 256, tt, b)
                    yield
                    ti = tmp_rot.next()
                    add("act", lambda e: e.activation(out=tA[ti][:, 0:256], in_=bk(b)[:, 0:256], func=AF.Tanh, scale=0.5),
                        r=[("ps", b)], w=[("tA", ti)])
                    yield
                    add("dve", lambda e: e.scalar_tensor_tensor(out=gate_ret[:, tt, :], in0=tA[ti][:, 0:256], scalar=1.0,
                                                                 in1=gate_ret[:, tt, :], op0=ALU.add, op1=ALU.mult),
                        r=[("tA", ti), ("gr", tt)], w=[("gr", tt)])
                return u
            for tt in range(NT):
                units.append((u_g(tt), 3))
            return units

        def proj_units_D(h, st_, sl_):
            QT, K0T, K1T = qsets[st_]
            cg = h // 2
            units = []
            wh = {}

            def getw():
                if "w" not in wh:
                    wh["w"] = wstream.take(("in", cg * 5 + 3 + (h % 2)))
                    wstream.prefetch()
                    add("pool", lambda e: e.memset(VA[sl_][:, :, 128:129], 1.0), r=[], w=[("VAone", sl_)])
                return wh["w"]

            def u_qk(tb, which):
                def u():
                    wsl = getw()
                    tok = slice(tb * 512, (tb + 1) * 512)
                    b1 = pj.next()
                    yield from g_proj_fm(wsl, 128 * which, tb, b1)
                    yield
                    ti = tmp_rot.next()
                    add("act", lambda e: e.activation(out=sqb[ti], in_=bk(b1), func=AF.Square), r=[("ps", b1)], w=[("sqb", ti)])
                    yield
                    yield
                    b2 = pj.next()
                    add("pe", lambda e: e.matmul(bk(b2), lhsT=blk, rhs=sqb[ti], start=True, stop=True),
                        r=[("sqb", ti), ("c_blk",)], w=[("ps", b2)])
                    yield
                    yield
                    add("act", lambda e: e.activation(out=tA[ti], in_=bk(b2), func=AF.Ln, bias=EPS), r=[("ps", b2)], w=[("tA", ti)])
                    yield
                    yield
                    add("act", lambda e: e.activation(out=tB[ti], in_=tA[ti], func=AF.Exp, scale=-0.5), r=[("tA", ti)], w=[("tB", ti)])
                    yield
                    yield
                    if which == 0:
                        add("dve", lambda e: e.scalar_tensor_tensor(out=QT[:, tok], in0=bk(b1), scalar=gcols[:, 0:1], in1=tB[ti],
                                                                     op0=ALU.mult, op1=ALU.mult),
                            r=[("ps", b1), ("tB", ti), ("c_gc",)], w=[("QT", st_, tb)])
                    else:
                        add("dve", lambda e: e.scalar_tensor_tensor(out=K0T[:, tok], in0=bk(b1), scalar=gcols[:, 1:2], in1=tB[ti],
                                                                     op0=ALU.mult, op1=ALU.mult),
                            r=[("ps", b1), ("tB", ti), ("c_gc",)], w=[("K0T", st_, tb)])
                        add("dve", lambda e: e.scalar_tensor_tensor(out=K1T[:, tok], in0=bk(b1), scalar=gcols[:, 2:3], in1=tB[ti],
                                                                     op0=ALU.mult, op1=ALU.mult),
                            r=[("ps", b1), ("tB", ti), ("c_gc",)], w=[("K1T", st_, tb)])
                return u
            for tb in range(4):
                units.append((u_qk(tb, 0), 8))
                units.append((u_qk(tb, 1), 8))

            def u_v(tt):
                def u():
                    wsl = getw()
                    b = pj.next()
                    yield from g_proj_tm(wsl, 256, 256, tt, b)
                    yield
                    ti = tmp_rot.next()
                    add("act", lambda e: e.activation(out=tA[ti][:, 0:128], in_=bk(b)[:, 128:256], func=AF.Tanh, scale=0.5),
                        r=[("ps", b)], w=[("tA", ti)])
                    yield
                    add("dve", lambda e: e.tensor_copy(VA[sl_][:, tt, 0:128], bk(b)[:, 0:128]), r=[("ps", b), ("tA", ti)], w=[("VA", sl_, tt)])
                    add("dve", lambda e: e.scalar_tensor_tensor(out=gate_diff[sl_][:, tt, :], in0=tA[ti][:, 0:128], scalar=1.0,
                                                                 in1=sg8[:, 0:128], op0=ALU.add, op1=ALU.mult),
                        r=[("tA", ti), ("c_sg8",)], w=[("gd", sl_, tt)])
                return u
            for tt in range(NT):
                units.append((u_v(tt), 4))
            return units

        def attn_units_R(r_, st_, spawn):
            QT, K0T, K1T = qsets[st_]
            qfT = K1T
            units = []

            def SFb(c):
                return E[0][:, c, 0:256]

            def SBb(c):
                return E[0][:, c, 256:512]

            def prep(g):
                def u():
                    if g == 0:
                        add("pool", lambda e: e.memset(SFb(0), 0.0), r=[], w=[("E", 0, 0, "f")])
                        add("pool", lambda e: e.memset(SBb(15), 0.0), r=[], w=[("E", 0, 15, "b")])
                    b = sb.next()
                    pt = bk16(b)[:, 0:512].rearrange("p (k n) -> p k n", k=4)

                    def tr(e):
                        ins = None
                        for k in range(4):
                            c = 4 * g + k
                            ins = e.transpose(out=pt[:, k, :], in_=K0T[:, c * 128:(c + 1) * 128], identity=ident)
                        return ins
                    add("pe", tr, r=[("K0T", st_, g), ("c_ident",)], w=[("ps", b)])
                    add("act", lambda e: e.activation(out=E[1][:, 4 * g:4 * g + 4, 128:256], in_=pt, func=AF.Copy, scale=zt[:, r_:r_ + 1]),
                        r=[("ps", b), ("c_zt", r_)], w=[("E", 1, 4 * g + k, "kf") for k in range(4)])
                    add("act", lambda e: e.activation(out=E[1][:, 4 * g:4 * g + 4, 256:384], in_=pt, func=AF.Copy, scale=zt[:, 4 + r_:5 + r_]),
                        r=[("ps", b), ("c_zt", 4 + r_)], w=[("E", 1, 4 * g + k, "kb") for k in range(4)])
                    qv = QT[:, g * 512:(g + 1) * 512].rearrange("p (k n) -> p k n", k=4)
                    add("dve", lambda e: e.tensor_tensor(out=qfT[:, g * 512:(g + 1) * 512].rearrange("p (k n) -> p k n", k=4), in0=qv,
                                                          in1=Xi[:, r_, :].unsqueeze(1).to_broadcast([128, 4, 128]), op=ALU.mult),
                        r=[("QT", st_, g), ("c_xi", r_)], w=[("K1T", st_, g)])
                    add("dve", lambda e: e.tensor_tensor(out=E[1][:, 4 * g:4 * g + 4, 384:512], in0=qv,
                                                          in1=Xi[:, 4 + r_, :].unsqueeze(1).to_broadcast([128, 4, 128]), op=ALU.mult),
                        r=[("QT", st_, g), ("c_xi", 4 + r_)], w=[("E", 1, 4 * g + k, "qb") for k in range(4)])
                return u

            def hk_load(hh):
                def u():
                    src = bass.AP(gvb_t, 767 * (2 * r_ + hh), [[1, 128], [1, 640]])
                    dma("pool", hkb1, src, [], [("hkb",)], "hk0")
                return u

            def hk_flip(hh):
                def u():
                    for (c0, n) in ((0, 512), (512, 128)):
                        b = sb.next()
                        add("pe", lambda e, b=b, c0=c0, n=n: e.matmul(bk(b)[:, 0:n], lhsT=antii, rhs=hkb1[:, c0:c0 + n], start=True, stop=True),
                            r=[("hkb",), ("c_antii",)], w=[("ps", b)])
                        add("act", lambda e, b=b, c0=c0, n=n: e.activation(out=ehk[hh][:, c0:c0 + n], in_=bk(b)[:, 0:n], func=AF.Exp),
                            r=[("ps", b)], w=[("ehk", hh, c0)])
                return u

            def scan(c):
                def u():
                    cb = 15 - c
                    bu = ab.next()

                    def um(e):
                        ins = None
                        if c <= 14:
                            ins = e.matmul(bk(bu)[:, 0:256], lhsT=E[1][:, c, 128:256], rhs=V[:, c, :], start=True, stop=True)
                        if cb >= 1:
                            ins = e.matmul(bk(bu)[:, 256:512], lhsT=E[1][:, cb, 256:384], rhs=V[:, cb, :], start=(c > 14), stop=True)
                        return ins
                    if c <= 14 or cb >= 1:
                        add("pe", um, r=[("E", 1, c, "kf"), ("E", 1, cb, "kb"), ("V", c), ("V", cb)], w=[("ps", bu)])
                    bs = sb.next()
                    add("pe", lambda e: e.matmul(bk(bs)[:, 0:128], lhsT=K0T[:, c * 128:(c + 1) * 128], rhs=QT[:, c * 128:(c + 1) * 128],
                                                 start=True, stop=True),
                        r=[("K0T", st_, c // 4), ("QT", st_, c // 4)], w=[("ps", bs)])
                    add("dve", lambda e: e.scalar_tensor_tensor(out=E[1][:, c, 0:128], in0=bk(bs)[:, 0:128], scalar=pws[:, r_, 0:1],
                                                                 in1=dn[:, r_, 256:384], op0=ALU.mult, op1=ALU.mult),
                        r=[("ps", bs), ("c_dn", r_), ("c_pws",)], w=[("E", 1, c, "p")])
                    if c <= 14:
                        add("dve", lambda e: e.scalar_tensor_tensor(out=SFb(c + 1), in0=SFb(c), scalar=cdc[:, r_:r_ + 1], in1=bk(bu)[:, 0:256],
                                                                     op0=ALU.mult, op1=ALU.add),
                            r=[("ps", bu), ("E", 0, c, "f"), ("c_cdc",)], w=[("E", 0, c + 1, "f")])
                    if cb >= 1:
                        add("dve", lambda e: e.scalar_tensor_tensor(out=SBb(cb - 1), in0=SBb(cb), scalar=cdc[:, 4 + r_:5 + r_],
                                                                     in1=bk(bu)[:, 256:512], op0=ALU.mult, op1=ALU.add),
                            r=[("ps", bu), ("E", 0, cb, "b"), ("c_cdc",)], w=[("E", 0, cb - 1, "b")])
                return u

            def post(b, tt):
                ti = pp_rot.next()
                ji = jk_rot.next()
                st = stat[:, 24 + 4 * ti:28 + 4 * ti]
                yield
                add("act", lambda e: e.activation(out=jk[ji], in_=bk(b)[:, 0:256], func=AF.Square, accum_out=st[:, 0:1]),
                    r=[("ps", b)], w=[("jk", ji), ("st1", ti, 0)])
                yield
                yield
                add("act", lambda e: e.activation(out=st[:, 1:2], in_=st[:, 0:1], func=AF.Ln, scale=1.0 / 256, bias=EPS),
                    r=[("st1", ti, 0)], w=[("st1", ti, 1)])
                yield
                add("act", lambda e: e.activation(out=st[:, 2:3], in_=st[:, 1:2], func=AF.Exp, scale=-0.5, bias=math.log(0.25)),
                    r=[("st1", ti, 1)], w=[("st1", ti, 2)])
                yield
                add("dve", lambda e: e.scalar_tensor_tensor(out=merged[:, tt, 256 * r_:256 * r_ + 256], in0=bk(b)[:, 0:256],
                                                             scalar=st[:, 2:3], in1=gate_ret[:, tt, :], op0=ALU.mult, op1=ALU.mult),
                    r=[("ps", b), ("st1", ti, 2), ("gr", tt)], w=[("mg", tt, r_)])

            def outc(c):
                def u():
                    b = ab.next()

                    def om(e):
                        e.matmul(bk(b)[:, 0:256], lhsT=E[1][:, c, 0:128], rhs=V[:, c, :], start=True, stop=False)
                        e.matmul(bk(b)[:, 0:256], lhsT=qfT[:, c * 128:(c + 1) * 128], rhs=SFb(c), start=False, stop=False)
                        return e.matmul(bk(b)[:, 0:256], lhsT=E[1][:, c, 384:512], rhs=SBb(c), start=False, stop=True)
                    add("pe", om, r=[("E", 1, c, "p"), ("E", 1, c, "qb"), ("K1T", st_, c // 4), ("E", 0, c, "f"), ("E", 0, c, "b"), ("V", c)],
                        w=[("ps", b)])
                    if os.environ.get("KPOST", "1") == "1":
                        spawn(post(b, c))
                return u

            krs = int(os.environ.get("KRS", "2"))
            units.append(hk_load(0))
            for g in range(4):
                units.append(prep(g))
            if krs >= 1:
                for c in range(NT):
                    units.append(scan(c))
                    if c == 7:
                        units.append(hk_flip(0))
                        units.append(hk_load(1))
            units.append(hk_flip(1))
            for _ in range(6):
                units.append(lambda: None)
            if krs >= 2:
                for c in range(NT - 1, -1, -1):
                    units.append(outc(c))
                    units.append(lambda: None)
            while len(units) < 150:
                units.append(lambda: None)
            return units

        def attn_units_D(h, st_, sl_, spawn):
            QT, K0T, K1T = qsets[st_]
            cg = h // 2
            units = []
            state = {}

            def qk_step(ib, jt):
                es = state[("es", ib)]
                it0 = 2 * ib
                b = sb.next()
                dlt = jt - it0
                near = -1 <= dlt <= 2
                if near:
                    c0 = 256 - 128 * dlt
                    hkw = ehk[h % 2][:, c0:c0 + 256]
                    biasarg = 0.0
                    bkey = ("ehk", h % 2)
                else:
                    col = (15 if dlt < 0 else 31) * 8 + h
                    biasarg = tabb[:, col:col + 1]
                    bkey = ("c_tabb",)
                    hk2 = None

                def qk(e):
                    e.matmul(bk(b)[:, 0:256], lhsT=K0T[:, jt * 128:(jt + 1) * 128], rhs=QT[:, ib * 256:(ib + 1) * 256],
                             start=True, stop=False)
                    return e.matmul(bk(b)[:, 256:512], lhsT=K1T[:, jt * 128:(jt + 1) * 128], rhs=QT[:, ib * 256:(ib + 1) * 256],
                                    start=False, stop=True)
                add("pe", qk, r=[("K0T", st_, jt // 4), ("K1T", st_, jt // 4), ("QT", st_, ib // 2)], w=[("ps", b)])
                if near:
                    add("act", lambda e: e.activation(out=E[es][:, jt, :], in_=bk(b), func=AF.Exp), r=[("ps", b)], w=[("E", es, jt)])

                    def mulw(e):
                        e.tensor_tensor(out=E[es][:, jt, 0:256], in0=E[es][:, jt, 0:256], in1=hkw, op=ALU.mult)
                        return e.tensor_tensor(out=E[es][:, jt, 256:512], in0=E[es][:, jt, 256:512], in1=hkw, op=ALU.mult)
                    add("pool", mulw, r=[bkey, ("E", es, jt)], w=[("E", es, jt)])
                else:
                    add("act", lambda e: e.activation(out=E[es][:, jt, :], in_=bk(b), func=AF.Exp, bias=biasarg),
                        r=[("ps", b), bkey], w=[("E", es, jt)])

            def post(b, tt):
                ti = pp_rot.next()
                ji = jk_rot.next()
                st = stat[:, 32 + 8 * ti:40 + 8 * ti]
                rs_ap = bass.AP(bk(b).tensor, bk(b).offset + 128, [list(bk(b).ap[0]), [129, 2]])
                yield
                add("dve", lambda e: e.reciprocal(out=st[:, 0:2], in_=rs_ap), r=[("ps", b)], w=[("st2", ti, 0)])
                yield
                add("dve", lambda e: e.tensor_tensor(out=st[:, 2:3], in0=st[:, 1:2], in1=NLAM, op=ALU.mult),
                    r=[("st2", ti, 0), ("c_nlam",)], w=[("st2", ti, 1)])
                add("dve", lambda e: e.tensor_scalar(out=tP[ti], in0=bk(b)[:, 0:128], scalar1=st[:, 0:1], scalar2=None, op0=ALU.mult),
                    r=[("ps", b), ("st2", ti, 0)], w=[("tP", ti)])
                yield
                add("dve", lambda e: e.scalar_tensor_tensor(out=osb[ti], in0=bk(b)[:, 129:257], scalar=st[:, 2:3],
                                                             in1=tP[ti], op0=ALU.mult, op1=ALU.add),
                    r=[("ps", b), ("st2", ti, 1), ("tP", ti)], w=[("osb", ti)])
                yield
                add("dve", lambda e: e.scalar_tensor_tensor(out=jk[ji][:, 0:128], in0=osb[ti], scalar=1.0, in1=osb[ti],
                                                             op0=ALU.mult, op1=ALU.mult, accum_out=st[:, 3:4]),
                    r=[("osb", ti)], w=[("jk", ji), ("st2", ti, 3)])
                yield
                yield
                add("act", lambda e: e.activation(out=st[:, 4:5], in_=st[:, 3:4], func=AF.Ln, scale=1.0 / 128, bias=EPS),
                    r=[("st2", ti, 3)], w=[("st2", ti, 4)])
                yield
                yield
                add("act", lambda e: e.activation(out=st[:, 5:6], in_=st[:, 4:5], func=AF.Exp, scale=-0.5, bias=math.log(0.5)),
                    r=[("st2", ti, 4)], w=[("st2", ti, 5)])
                yield
                yield
                add("dve", lambda e: e.scalar_tensor_tensor(out=tmpb[ti], in0=osb[ti], scalar=st[:, 5:6],
                                                             in1=gate_diff[sl_][:, tt, :], op0=ALU.mult, op1=ALU.mult),
                    r=[("osb", ti), ("st2", ti, 5), ("gd", sl_, tt)], w=[("tmpb", ti)])
                yield
                mcol = 128 * h
                add("pool", lambda e: e.tensor_tensor(out=merged[:, tt, mcol:mcol + 128], in0=merged[:, tt, mcol:mcol + 128],
                                                       in1=tmpb[ti], op=ALU.add),
                    r=[("tmpb", ti), ("mg", tt, cg)], w=[("mg", tt, cg)])

            def pv_piece(ib, step):
                es = state[("es", ib)]
                g = step // 4
                i2, mp = g // 2, g % 2
                j0 = (step % 4) * 4
                if step % 8 == 0:
                    state[("ab", ib, i2)] = ab.next()
                b = state[("ab", ib, i2)]

                def f(e):
                    ins = None
                    for jt in range(j0, j0 + 4):
                        ins = e.matmul(bk(b)[:, mp * 129:mp * 129 + 129],
                                       lhsT=E[es][:, jt, mp * 256 + i2 * 128:mp * 256 + (i2 + 1) * 128],
                                       rhs=VA[sl_][:, jt, 0:129], start=(jt == 0), stop=(jt == NT - 1))
                    return ins
                add("pe", f, r=[("E", es), ("VA", sl_), ("VAone", sl_)], w=[("ps", b)])
                if step % 8 == 7:
                    spawn(post(b, 2 * ib + i2))

            def mk(ib, jt):
                def u():
                    if ib < 8:
                        if jt == 0:
                            state[("es", ib)] = eslot.next()
                        qk_step(ib, jt)
                    if ib > 0:
                        pv_piece(ib - 1, jt)
                return u
            for ib in range(9):
                for jt in range(NT):
                    units.append(mk(ib, jt))
            return units

        active = []

        def spawn(g):
            active.append(g)

        def advance():
            for g in list(active):
                try:
                    next(g)
                except StopIteration:
                    active.remove(g)

        tasks = []
        dcount = 0
        for cg in range(4):
            tasks.append(("R", cg))
            tasks.append(("D", 2 * cg))
            tasks.append(("D", 2 * cg + 1))
        projs, attns = [], []
        for k, (kind, idx) in enumerate(tasks):
            st_ = k % 2
            if kind == "R":
                projs.append(proj_units_R(idx, st_))
                attns.append(attn_units_R(idx, st_, spawn))
            else:
                sl_ = dcount % 2
                dcount += 1
                projs.append(proj_units_D(idx, st_, sl_))
                attns.append(attn_units_D(idx, st_, sl_, spawn))
        def run_side_only(side):
            si = 0
            next_start = 0
            mi = 0
            while si < len(side) or active:
                advance()
                if si < len(side) and mi >= next_start:
                    g = side[si][0]()
                    next_start = mi + side[si][1]
                    si += 1
                    spawn(g)
                    try:
                        next(g)
                    except StopIteration:
                        active.remove(g)
                mi += 1
        if os.environ.get("KSO", "1") == "1":
            run_side_only(projs[0])
        else:
            for (u, per) in projs[0]:
                for _ in u():
                    pass
        for k in range(len(tasks)):
            main = attns[k]
            side = projs[k + 1] if k + 1 < len(tasks) else []
            si = 0
            next_start = 0
            for mi, u in enumerate(main):
                u()
                advance()
                if si < len(side) and mi >= next_start:
                    g = side[si][0]()
                    next_start = mi + side[si][1]
                    si += 1
                    spawn(g)
                    try:
                        next(g)
                    except StopIteration:
                        active.remove(g)
            while si < len(side):
                g = side[si][0]()
                si += 1
                for _ in g:
                    pass
            while active:
                advance()
        sch.barrier()

        if os.environ.get("KP2", "1") == "0":
            continue
        tb_rot = Rot([0, 1])
        for tt in range(NT):
            b = tb_rot.next()
            pt = bk16(b).rearrange("p (k n) -> p k n", k=8)

            def tr(e, tt=tt, pt=pt):
                ins = None
                for kc in range(8):
                    ins = e.transpose(out=pt[:, kc, :], in_=merged[:, tt, kc * 128:(kc + 1) * 128], identity=ident)
                return ins
            add("pe", tr, r=[("mg", tt), ("c_ident",)], w=[("ps", b)])
            eng = "act" if tt % 2 == 0 else "dve"
            if eng == "act":
                add("act", lambda e, pt=pt, tt=tt: e.activation(out=hT[:, :, tt * 128:(tt + 1) * 128], in_=pt, func=AF.Copy),
                    r=[("ps", b)], w=[("hT", tt)])
            else:
                add("dve", lambda e, pt=pt, tt=tt: e.tensor_copy(hT[:, :, tt * 128:(tt + 1) * 128], pt),
                    r=[("ps", b)], w=[("hT", tt)])
        sch.barrier()
        pj = Rot([2, 3])
        gu = Rot([4, 5, 6, 7])
        xs_rot = Rot([0, 1, 2, 3])
        ys_rot = Rot([0, 1])
        sg_rot = Rot([0, 1])
        wd_rot = Rot([0, 1])
        for qd in range(4):
            for c2 in range(2):
                wsl = wstream.take(("wo", c2))
                wstream.prefetch()
                for tl in range(4):
                    tt = 4 * qd + tl
                    xsl = xs_rot.next()
                    dma("sp", xs2[xsl], x_d[s, tt * 128:(tt + 1) * 128, 512 * c2:512 * c2 + 512], [], [("xs2", xsl)], "xs2_%d" % xsl)
                    b = pj.next()

                    def op(e, b=b, tt=tt, wsl=wsl):
                        ins = None
                        for kc in range(8):
                            ins = e.matmul(bk(b), lhsT=hT[:, kc, tt * 128:(tt + 1) * 128], rhs=wring[wsl][:, kc, :],
                                           start=(kc == 0), stop=(kc == 7))
                        return ins
                    add("pe", op, r=[("W", wsl), ("hT", tt)], w=[("ps", b)])
                    add("dve", lambda e, b=b, tl=tl, c2=c2, xsl=xsl: e.tensor_tensor(out=x1[:, tl, 512 * c2:512 * c2 + 512], in0=bk(b),
                                                                                   in1=xs2[xsl], op=ALU.add),
                        r=[("ps", b), ("xs2", xsl)], w=[("x1", tl, c2)])
            for tl in range(4):
                sl = tl % 2
                st = stat[:, 48 + 4 * sl:52 + 4 * sl]
                add("act", lambda e, tl=tl, st=st: e.activation(out=junk, in_=x1[:, tl, :], func=AF.Square, accum_out=st[:, 0:1]),
                    r=[("x1", tl)], w=[("junk",), ("st3", sl, 0)])
                add("act", lambda e, st=st: e.activation(out=st[:, 1:2], in_=st[:, 0:1], func=AF.Ln, scale=1.0 / D, bias=EPS),
                    r=[("st3", sl, 0)], w=[("st3", sl, 1)])
                add("act", lambda e, st=st: e.activation(out=st[:, 2:3], in_=st[:, 1:2], func=AF.Exp, scale=-0.5),
                    r=[("st3", sl, 1)], w=[("st3", sl, 2)])
                add("act", lambda e, tl=tl, sl=sl, st=st: e.activation(out=xh2[sl], in_=x1[:, tl, :], func=AF.Copy, scale=st[:, 2:3]),
                    r=[("x1", tl), ("st3", sl, 2)], w=[("xh2", sl)])
                b = tb_rot.next()
                pt = bk16(b).rearrange("p (k n) -> p k n", k=8)

                def tr(e, sl=sl, pt=pt):
                    ins = None
                    for kc in range(8):
                        ins = e.transpose(out=pt[:, kc, :], in_=xh2[sl][:, kc * 128:(kc + 1) * 128], identity=ident)
                    return ins
                add("pe", tr, r=[("xh2", sl), ("c_ident",)], w=[("ps", b)])
                add("dve", lambda e, pt=pt, tl=tl: e.tensor_tensor(out=h2T[:, :, tl * 128:(tl + 1) * 128], in0=pt,
                                                                    in1=g2t.unsqueeze(2).to_broadcast([128, 8, 128]), op=ALU.mult),
                    r=[("ps", b), ("c_g2t",)], w=[("h2T", tl)])
            for g in range(11):
                wsl = wstream.take(("gu", g))
                wstream.prefetch()
                for fl in range(2):
                    fc = 2 * g + fl
                    bg = gu.next()
                    bu = gu.next()

                    def gup(e, wsl=wsl, fl=fl, bg=bg, bu=bu):
                        ins = None
                        for kc in range(8):
                            e.matmul(bk(bg), lhsT=wring[wsl][:, kc, fl * 128:(fl + 1) * 128], rhs=h2T[:, kc, :], start=(kc == 0), stop=(kc == 7))
                        for kc in range(8):
                            ins = e.matmul(bk(bu), lhsT=wring[wsl][:, kc, 256 + fl * 128:256 + (fl + 1) * 128], rhs=h2T[:, kc, :],
                                           start=(kc == 0), stop=(kc == 7))
                        return ins
                    add("pe", gup, r=[("W", wsl), ("h2T",)], w=[("ps", bg), ("ps", bu)])
                    si = sg_rot.next()
                    add("act", lambda e, bg=bg, si=si: e.activation(out=sgb[si], in_=bk(bg), func=AF.Silu),
                        r=[("ps", bg)], w=[("sgb", si)])
                    add("dve", lambda e, bu=bu, si=si, fc=fc: e.tensor_tensor(out=actT[:, fc, :], in0=bk(bu), in1=sgb[si], op=ALU.mult),
                        r=[("ps", bu), ("sgb", si)], w=[("actT", fc)])
            for c2 in range(2):
                wdl = wd_rot.next()
                dma("sp", WD[wdl], wd_s[c2].rearrange("(f p) c -> p f c", p=128), [], [("WD", wdl)], "wd%d" % wdl)
                for tl in range(4):
                    tt = 4 * qd + tl
                    b = pj.next()

                    def dn_(e, b=b, tl=tl, wdl=wdl):
                        ins = None
                        for fc in range(NFC):
                            ins = e.matmul(bk(b), lhsT=actT[:, fc, tl * 128:(tl + 1) * 128], rhs=WD[wdl][:, fc, :],
                                           start=(fc == 0), stop=(fc == NFC - 1))
                        return ins
                    add("pe", dn_, r=[("WD", wdl), ("actT",)], w=[("ps", b)])
                    ysl = ys_rot.next()
                    add("dve", lambda e, b=b, tl=tl, c2=c2, ysl=ysl: e.tensor_tensor(out=yst[ysl], in0=bk(b), in1=x1[:, tl, 512 * c2:512 * c2 + 512],
                                                                                   op=ALU.add),
                        r=[("ps", b), ("x1", tl, c2)], w=[("yst", ysl)])
                    dma("pool", y_d[s, tt * 128:(tt + 1) * 128, 512 * c2:512 * c2 + 512], yst[ysl], [("yst", ysl)], [("ydram", ysl)], "yo%d" % ysl)
        sch.barrier()

    add("pool", None, r=[("ydram",)], w=[])
    sch.emit(nc, stack)
    stack.close()
    return nc


_PROG_CACHE = {}


def _run(xs_per_core, wts, consts):
    nseq = xs_per_core[0].shape[0]
    if nseq not in _PROG_CACHE:
        _PROG_CACHE[nseq] = build_program(nseq)
    nc = _PROG_CACHE[nseq]
    in_maps = []
    for c in range(NCORES):
        m = dict(wts)
        m.update(consts)
        m["x"] = xs_per_core[c]
        in_maps.append(m)
    res = run_bass_kernel_spmd(nc, in_maps, core_ids=list(range(NCORES)))
    return [r["y"] for r in res.results]


def kernel(x_prompt, x_sample, rel_bias_table, norm_mix_g, w_in, ret_decay_fwd, ret_decay_bwd,
           q_norm_g, k_norm_g, lam_q1, lam_k1, lam_q2, lam_k2, subln_g, w_out, norm_ffn_g,
           w_gate, w_up, w_down):
    f32 = np.float32
    A = lambda a: np.ascontiguousarray(np.asarray(a, dtype=f32))
    x_prompt = A(x_prompt)
    x_sample = A(x_sample)
    consts = _host_consts()
    wts = {
        "w_in": A(w_in)[0], "w_out": A(w_out)[0], "w_gate": A(w_gate)[0], "w_up": A(w_up)[0], "w_down": A(w_down)[0],
        "tab": A(rel_bias_table),
        "tabb": A(np.broadcast_to(A(rel_bias_table).reshape(1, 256), (128, 256))),
        "dec": A(np.broadcast_to(np.concatenate([A(ret_decay_fwd)[0], A(ret_decay_bwd)[0]])[None, :], (128, 8))),
        "lamv": A(np.broadcast_to(np.stack([A(lam_q1)[0], A(lam_q2)[0], A(lam_k1)[0], A(lam_k2)[0]])[None], (128, 4, 64))),
        "gqk": A(np.stack([np.tile(A(q_norm_g)[0], 2), np.tile(A(k_norm_g)[0], 2)], axis=1)),
        "subln": A(np.broadcast_to(np.tile(A(subln_g)[0], 2)[None, :], (128, 256))),
        "g1t": A(A(norm_mix_g)[0].reshape(8, 128).T),
        "g2t": A(A(norm_ffn_g)[0].reshape(8, 128).T),
    }
    nP = x_prompt.shape[0] // NCORES
    nS = x_sample.shape[0] // NCORES
    xs = [np.ascontiguousarray(np.concatenate([x_prompt[c * nP:(c + 1) * nP], x_sample[c * nS:(c + 1) * nS]], axis=0))
          for c in range(NCORES)]
    ys = _run(xs, wts, consts)
    y_prompt = np.concatenate([y[:nP] for y in ys], axis=0).astype(f32)
    y_sample = np.concatenate([y[nP:] for y in ys], axis=0).astype(f32)
    return (y_prompt, y_sample)
```
